# Optimizing a Trainium2 kernel written in Bass

```python
import math
import jax, jax.numpy as jnp
from jax import lax
import numpy as np

D_MODEL = 1024
BATCH = 8
SEQ = 4096
DEPTH = 1

HEAD_DIM = 64
N_HEADS = D_MODEL // HEAD_DIM
N_KV_HEADS = N_HEADS // 4
ATTN_WIDTH = N_HEADS * HEAD_DIM
KV_WIDTH = N_KV_HEADS * HEAD_DIM
WINDOW = 128
BLOCK = 128
ROT_DIM = HEAD_DIM // 4
ROPE_THETA = 500000.0
RNN_WIDTH = D_MODEL
RNN_BLOCK_WIDTH = 256
RNN_N_BLOCKS = RNN_WIDTH // RNN_BLOCK_WIDTH
LRU_C = 8.0
CONV_WIDTH = 4
NORM_EPS = 1e-6
SPLIT_SIZES = (ATTN_WIDTH, KV_WIDTH, KV_WIDTH, ATTN_WIDTH,
               RNN_WIDTH, RNN_WIDTH,
               D_MODEL, D_MODEL)
IN_WIDTH = sum(SPLIT_SIZES)
SPLIT_POINTS = tuple(int(v) for v in np.cumsum(SPLIT_SIZES)[:-1])

kernel_name = "hybrid_swa_sink_rglru_gated_block"


def rmsnorm(x, g):
    xf = x.astype(jnp.float32)
    r = lax.rsqrt(jnp.mean(xf * xf, axis=-1, keepdims=True) + NORM_EPS)
    return (xf * r).astype(x.dtype) * g


def partial_rope(t, pos):
    half = ROT_DIM // 2
    inv_freq = ROPE_THETA ** (-jnp.arange(0, ROT_DIM, 2, dtype=jnp.float32) / ROT_DIM)
    ang = pos[..., None].astype(jnp.float32) * inv_freq
    cos = jnp.cos(ang)[:, :, None, :]
    sin = jnp.sin(ang)[:, :, None, :]
    rot = t[..., :ROT_DIM].astype(jnp.float32)
    x1, x2 = rot[..., :half], rot[..., half:]
    rotated = jnp.concatenate([x1 * cos - x2 * sin, x2 * cos + x1 * sin], axis=-1)
    return jnp.concatenate([rotated.astype(t.dtype), t[..., ROT_DIM:]], axis=-1)


def sliding_window_attention_with_sinks(q, k, v, sinks):
    B, S, H, hd = q.shape
    nb = S // BLOCK
    G = H // N_KV_HEADS
    qb = q.reshape(B, nb, BLOCK, N_KV_HEADS, G, hd).astype(jnp.float32)
    pad = ((0, 0), (BLOCK, 0), (0, 0), (0, 0))
    kp = jnp.pad(k, pad).reshape(B, nb + 1, BLOCK, N_KV_HEADS, hd)
    vp = jnp.pad(v, pad).reshape(B, nb + 1, BLOCK, N_KV_HEADS, hd)
    kb = jnp.concatenate([kp[:, :-1], kp[:, 1:]], axis=2).astype(jnp.float32)
    vb = jnp.concatenate([vp[:, :-1], vp[:, 1:]], axis=2).astype(jnp.float32)
    s = jnp.einsum('bnqkgd,bnskd->bnkgqs', qb, kb) * (1.0 / math.sqrt(hd))
    qi = jnp.arange(BLOCK)[:, None]
    kj = jnp.arange(2 * BLOCK)[None, :]
    diff = qi + BLOCK - kj
    band = (diff >= 0) & (diff < WINDOW)
    kpos = jnp.arange(nb)[:, None] * BLOCK - BLOCK + jnp.arange(2 * BLOCK)[None, :]
    mask = band[None] & (kpos >= 0)[:, None, :]
    s = jnp.where(mask[None, :, None, None], s, -1e30)
    sink = sinks.astype(jnp.float32).reshape(N_KV_HEADS, G)[None, None, :, :, None, None]
    m = jnp.maximum(jnp.max(s, axis=-1, keepdims=True), sink)
    p = jnp.exp(s - m)
    denom = jnp.sum(p, axis=-1, keepdims=True) + jnp.exp(sink - m)
    o = jnp.einsum('bnkgqs,bnskd->bnqkgd', p / denom, vb)
    return o.reshape(B, S, H * hd).astype(q.dtype)


def causal_depthwise_conv(x, w, b):
    S = x.shape[1]
    xp = jnp.pad(x, ((0, 0), (CONV_WIDTH - 1, 0), (0, 0)))
    y = sum(xp[:, k:k + S] * w[k] for k in range(CONV_WIDTH))
    return y + b


def rg_lru(xr, pos, wa, ba, wx, bx, lam):
    B, S, D = xr.shape
    xb = xr.reshape(B, S, RNN_N_BLOCKS, RNN_BLOCK_WIDTH)
    r = jax.nn.sigmoid(jnp.einsum('bshi,hij->bshj', xb, wa).reshape(B, S, D) + ba)
    i = jax.nn.sigmoid(jnp.einsum('bshi,hij->bshj', xb, wx).reshape(B, S, D) + bx)
    log_a = -LRU_C * r.astype(jnp.float32) * jax.nn.softplus(-lam.astype(jnp.float32))
    a = jnp.exp(log_a)
    mult = jnp.sqrt(-jnp.expm1(2.0 * log_a))
    reset = (pos == 0)[..., None]
    mult = jnp.where(reset, 1.0, mult)
    a = jnp.where(reset, 0.0, a)
    b = mult * (i * xr).astype(jnp.float32)

    def combine(lhs, rhs):
        a1, b1 = lhs
        a2, b2 = rhs
        return a1 * a2, a2 * b1 + b2

    _, h = lax.associative_scan(combine, (a, b), axis=1)
    return h.astype(xr.dtype)


def setup_inputs(seed: int = 0) -> dict:
    key = jax.random.key(seed)
    ks = jax.random.split(key, 20)
    f32 = jnp.float32
    nrm = lambda k, shape, scale: jax.random.normal(k, shape, f32) * scale
    s = jax.nn.sigmoid(jnp.zeros(()))
    del s
    a_c = jax.random.uniform(ks[12], (DEPTH, RNN_WIDTH), f32, 0.9, 0.999)
    a_base = a_c ** (1.0 / LRU_C)
    lam = jnp.log(a_base) - jnp.log1p(-a_base)
    return {
        "x": nrm(ks[0], (BATCH, SEQ, D_MODEL), 1.0),
        "c": nrm(ks[1], (BATCH, D_MODEL), 1.0),
        "positions": jnp.broadcast_to(jnp.arange(SEQ, dtype=jnp.int32), (BATCH, SEQ)),
        "w_ada": nrm(ks[2], (DEPTH, D_MODEL, 3 * D_MODEL), 0.1 * D_MODEL ** -0.5),
        "b_ada": nrm(ks[3], (DEPTH, 3 * D_MODEL), 0.01),
        "norm_g": 1.0 + nrm(ks[4], (DEPTH, D_MODEL), 0.02),
        "w_in": nrm(ks[5], (DEPTH, D_MODEL, IN_WIDTH), D_MODEL ** -0.5),
        "attn_sinks": nrm(ks[6], (DEPTH, N_HEADS), 0.5),
        "conv_w": nrm(ks[7], (DEPTH, CONV_WIDTH, RNN_WIDTH), CONV_WIDTH ** -0.5),
        "conv_b": nrm(ks[8], (DEPTH, RNN_WIDTH), 0.01),
        "rg_wa": nrm(ks[9], (DEPTH, RNN_N_BLOCKS, RNN_BLOCK_WIDTH, RNN_BLOCK_WIDTH), RNN_BLOCK_WIDTH ** -0.5),
        "rg_ba": nrm(ks[10], (DEPTH, RNN_WIDTH), 0.01),
        "rg_wx": nrm(ks[11], (DEPTH, RNN_N_BLOCKS, RNN_BLOCK_WIDTH, RNN_BLOCK_WIDTH), RNN_BLOCK_WIDTH ** -0.5),
        "rg_bx": nrm(ks[13], (DEPTH, RNN_WIDTH), 0.01),
        "rg_lambda": lam,
        "w_attn_proj": nrm(ks[14], (DEPTH, ATTN_WIDTH, D_MODEL), ATTN_WIDTH ** -0.5),
        "w_rnn_proj": nrm(ks[15], (DEPTH, RNN_WIDTH, D_MODEL), RNN_WIDTH ** -0.5),
        "w_out": nrm(ks[16], (DEPTH, D_MODEL, D_MODEL), D_MODEL ** -0.5),
        "final_g": 1.0 + nrm(ks[17], (D_MODEL,), 0.02),
    }


def reference(x, c, positions, w_ada, b_ada, norm_g, w_in, attn_sinks, conv_w, conv_b,
              rg_wa, rg_ba, rg_wx, rg_bx, rg_lambda, w_attn_proj, w_rnn_proj, w_out, final_g):
    B, S, _ = x.shape
    for l in range(DEPTH):
        mod = c @ w_ada[l] + b_ada[l]
        shift, scale, gate = jnp.split(mod, 3, axis=-1)
        h = rmsnorm(x, norm_g[l]) * (1.0 + scale[:, None, :]) + shift[:, None, :]
        proj = h @ w_in[l]
        q, k, v, g_attn, xr, g_rnn, m_attn, m_rnn = jnp.split(proj, SPLIT_POINTS, axis=-1)
        q = partial_rope(q.reshape(B, S, N_HEADS, HEAD_DIM), positions)
        k = partial_rope(k.reshape(B, S, N_KV_HEADS, HEAD_DIM), positions)
        v = v.reshape(B, S, N_KV_HEADS, HEAD_DIM)
        y_attn = sliding_window_attention_with_sinks(q, k, v, attn_sinks[l]) * jax.nn.silu(g_attn)
        xr = causal_depthwise_conv(xr, conv_w[l], conv_b[l])
        y_rnn = rg_lru(xr, positions, rg_wa[l], rg_ba[l], rg_wx[l], rg_bx[l], rg_lambda[l]) * jax.nn.silu(g_rnn)
        merged = (jax.nn.sigmoid(m_attn) * (y_attn @ w_attn_proj[l])
                  + jax.nn.sigmoid(m_rnn) * (y_rnn @ w_rnn_proj[l]))
        x = x + gate[:, None, :] * (merged @ w_out[l])
    return rmsnorm(x, final_g)
```

```python
import math
import numpy as np
import ml_dtypes
import concourse.bass as bass
import concourse.mybir as mybir
from concourse.bass_utils import run_bass_kernel_spmd

F32 = mybir.dt.float32
BF16 = mybir.dt.bfloat16
I32 = mybir.dt.int32
AF = mybir.ActivationFunctionType
ALU = mybir.AluOpType

D = 1024
KC = 8
TT = 512
NH = 16
NKV = 4
HD = 64
IN_W = 6656
NPIECE = 19
PC_QKV = [0, 1, 2]
PC_GA = [3, 4]
PC_XR = [5, 6]
PC_GR = [7, 8]
PC_MA = [9, 10]
PC_MR = [11, 12]
PC_WAP = [13, 14]
PC_WRP = [15, 16]
PC_WO = [17, 18]
NRING = 3
KRING = 8
EPS = 1e-6
TWO_PI = 2.0 * math.pi
C1 = 6.28125
C2 = TWO_PI - C1
LN_HALF = math.log(0.5)

PP_CT, PP_BADA, PP_NG, PP_CW, PP_CB, PP_BA, PP_BX, PP_LAM, PP_SINK, PP_INVF = 0, 8, 32, 40, 72, 80, 88, 96, 104, 112
NPP = 120
_STOP = 99
_PRO = 99
_SUB = 99


class Prog:
    LIMIT = 10 ** 9
    LOG = None
    SCHED = True
    DEFC = {"sync": 3.0, "act": 0.62, "dve": 0.6, "pool": 0.9, "pe": 2.05}
    TBL_SWITCH = 6.0
    TBL_BONUS = 100.0
    WINDOW = 1300

    def __init__(self):
        self.ops = []
        self.last_w = {}
        self.readers = {}
        self.real_w = {}
        self.next_c = None

    def add(self, eng, fn, r=(), w=(), dma=None, c=None, tbl=None):
        idx = len(self.ops)
        if idx >= Prog.LIMIT:
            return -1
        if Prog.LOG is not None:
            import sys
            Prog.LOG.append((idx, eng, sys._getframe(1).f_lineno, sys._getframe(2).f_lineno))
        raw = set()
        for k in r:
            if k in self.real_w:
                raw.add(self.real_w[k])
        for k in w:
            self.real_w[k] = idx
        w = list(w) + [k for k in r if k.startswith("ps") and k not in w]
        r = [k for k in r if not k.startswith("ps")]
        deps = set()
        for k in r:
            if k in self.last_w:
                deps.add(self.last_w[k])
        for k in w:
            if k in self.last_w:
                deps.add(self.last_w[k])
            for rd in self.readers.get(k, ()):
                deps.add(rd)
        for k in r:
            self.readers.setdefault(k, []).append(idx)
        for k in w:
            self.last_w[k] = idx
            self.readers[k] = []
        deps.discard(idx)
        raw.discard(idx)
        if c is None and self.next_c is not None:
            c = self.next_c
        self.next_c = None
        self.ops.append(dict(eng=eng, fn=fn, deps=deps, raw=raw, dma=dma, tbl=tbl, c=(Prog.DEFC[eng] if c is None else c)))
        return idx

    def schedule(self):
        ops = self.ops
        n = len(ops)
        if not Prog.SCHED:
            return list(range(n))
        succ = [[] for _ in range(n)]
        ndep = [0] * n
        for i, o in enumerate(ops):
            ndep[i] = len(o["deps"])
            for d in o["deps"]:
                succ[d].append(i)
        engs = ["sync", "act", "dve", "pool", "pe"]
        free_at = {e: 0.0 for e in engs}
        ready = {e: [] for e in engs}
        rtime = [0.0] * n
        fin = [0.0] * n
        start_t = [0.0] * n
        blocker = [-1] * n
        for i in range(n):
            if ndep[i] == 0:
                ready[ops[i]["eng"]].append(i)
        blevel = [0.0] * n
        for i in range(n - 1, -1, -1):
            m = 0.0
            for j in succ[i]:
                if blevel[j] > m:
                    m = blevel[j]
            blevel[i] = m + ops[i]["c"]
        order = []
        WINDOW = Prog.WINDOW
        cur_tbl = [None]
        nxt = 0
        done = [False] * n
        while len(order) < n:
            while nxt < n and done[nxt]:
                nxt += 1
            best = None
            for e in engs:
                cands = [i for i in ready[e] if i <= nxt + WINDOW]
                if not cands:
                    continue
                t_e = max(free_at[e], min(rtime[i] for i in cands))
                pick = None
                for i in cands:
                    if rtime[i] > t_e + 1e-9:
                        continue
                    pr = blevel[i]
                    tb = ops[i]["tbl"]
                    if e == "act" and (tb is None or tb == cur_tbl[0]):
                        pr += Prog.TBL_BONUS
                    if pick is None or pr > pick[0] or (pr == pick[0] and i < pick[1]):
                        pick = (pr, i)
                i = pick[1]
                st = max(rtime[i], free_at[e])
                tb = ops[i]["tbl"]
                if e == "act" and tb is not None and tb != cur_tbl[0]:
                    st += Prog.TBL_SWITCH
                key = (st, i)
                if best is None or key < best[0]:
                    best = (key, e, i)
            if best is None:
                for e in engs:
                    for i in ready[e]:
                        key = (max(rtime[i], free_at[e]), i)
                        if best is None or i < best[2]:
                            best = (key, e, i)
            (st, _), e, i = best
            ready[e].remove(i)
            o = ops[i]
            if e == "act" and o["tbl"] is not None:
                cur_tbl[0] = o["tbl"]
            if o["dma"] is not None:
                free_at[e] = st + 0.15
                fin[i] = st + o["c"]
            else:
                fin[i] = st + o["c"]
                free_at[e] = fin[i]
            done[i] = True
            order.append(i)
            start_t[i] = st
            for j in succ[i]:
                ndep[j] -= 1
                if fin[i] > rtime[j]:
                    blocker[j] = i
                rtime[j] = max(rtime[j], fin[i])
                if ndep[j] == 0:
                    ready[ops[j]["eng"]].append(j)
        self.est_us = max(fin) if n else 0.0
        self.sim = (start_t, fin, blocker, order)
        return order

    def emit(self, nc, block, stack):
        ops = self.ops
        engs = ["sync", "act", "dve", "pool", "pe"]
        order = self.schedule()
        chained = ("act", "dve", "pool")

        def needs_wait(o, p, d):
            if p["dma"] is not None:
                return True
            if p["eng"] != o["eng"]:
                return True
            return p["eng"] in chained

        for o in ops:
            o["sig"] = False
        for o in ops:
            for d in o["deps"]:
                p = ops[d]
                if p["dma"] is None and needs_wait(o, p, d):
                    p["sig"] = True
        esem = {e: stack.enter_context(nc.semaphore("e_" + e)) for e in engs}
        dsem = {}
        dcount = {}
        ecount = {e: 0 for e in engs}
        for i in order:
            o = ops[i]
            if o["dma"] is not None:
                k = o["dma"]
                if k not in dsem:
                    dsem[k] = stack.enter_context(nc.semaphore("d_" + k))
                    dcount[k] = 0
                dcount[k] += 16
                o["sval"] = (dsem[k], dcount[k], "d_" + k)
            elif o["sig"]:
                ecount[o["eng"]] += 1
                o["sval"] = (esem[o["eng"]], ecount[o["eng"]], "e_" + o["eng"])
        per_eng = {e: [] for e in engs}
        for i in order:
            per_eng[ops[i]["eng"]].append(ops[i])

        def run(eng_name, eng):
            waited = {}
            for o in per_eng[eng_name]:
                need = {}
                for d in o["deps"]:
                    p = ops[d]
                    if not needs_wait(o, p, d):
                        continue
                    sem, val, name = p["sval"]
                    if need.get(name, (None, 0))[1] < val:
                        need[name] = (sem, val)
                for name, (sem, val) in need.items():
                    if waited.get(name, 0) < val:
                        eng.wait_ge(sem, val)
                        waited[name] = val
                if o["fn"] is None:
                    continue
                ins = o["fn"](eng)
                if o["dma"] is not None:
                    ins.then_inc(o["sval"][0], 16)
                elif o["sig"]:
                    ins.then_inc(o["sval"][0], 1)

        @block.sync
        def _(e):
            run("sync", e)

        @block.scalar
        def _(e):
            run("act", e)

        @block.vector
        def _(e):
            run("dve", e)

        @block.gpsimd
        def _(e):
            run("pool", e)

        @block.tensor
        def _(e):
            run("pe", e)


def build(S):
    from contextlib import ExitStack
    NB = S // 128
    NT = S // TT
    nc = bass.Bass("TRN2", target_bir_lowering=False)
    dt = nc.dram_tensor
    x_d = dt("x", [S, D], F32, kind="ExternalInput").ap()
    posT_d = dt("posT", [128, NB], I32, kind="ExternalInput").ap()
    posrow_d = dt("posrow", [1, S], I32, kind="ExternalInput").ap()
    pp_d = dt("pp", [128, NPP], F32, kind="ExternalInput").ap()
    rows_d = dt("rows", [2, D], F32, kind="ExternalInput").ap()
    cbf_d = dt("cbf", [128, 128 + 1024 + 64], BF16, kind="ExternalInput").ap()
    wada_d = dt("w_ada", [D, 3 * D], F32, kind="ExternalInput").ap()
    win_d = dt("w_in", [D, IN_W], F32, kind="ExternalInput").ap()
    wap_d = dt("w_ap", [D, D], F32, kind="ExternalInput").ap()
    wrp_d = dt("w_rp", [D, D], F32, kind="ExternalInput").ap()
    wo_d = dt("w_o", [D, D], F32, kind="ExternalInput").ap()
    rgw_d = dt("rgw", [128, 8, 512], F32, kind="ExternalInput").ap()
    scr_d = dt("wscr", [NPIECE, 128, KC, 512], BF16, kind="Internal").ap()
    y_d = dt("y", [S, D], F32, kind="ExternalOutput").ap()

    P = Prog()
    with ExitStack() as st:
        sb = lambda name, shape, dtype: st.enter_context(nc.sbuf_tensor("s_" + name, shape, dtype))
        ps = st.enter_context(nc.psum_tensor("ps", [128, 8, 512], F32))
        pp = sb("pp", [128, NPP], F32)
        cbf = sb("cbf", [128, 128 + 1024 + 64], BF16)
        ident = cbf[:, 0:128]
        maskb = cbf[:, 128:128 + 1024]
        onesb = cbf[:, 1152:1216]
        fgbc = sb("fgbc", [128, D], F32)
        modpp = sb("modpp", [128, 24], F32)
        gpp = sb("gpp", [128, 8], F32)
        small = sb("small", [128, 64], F32)
        cp = small[:, 0:8]
        c2p = small[:, 8:16]
        espp = small[:, 16:24]
        tmp8 = small[:, 24:32]
        hstate = small[:, 32:40]
        nba = small[:, 40:48]
        nbx = small[:, 48:56]
        nhalf = small[:, 56:57]
        halo = sb("halo", [128, 8, 3], F32)
        posTi = sb("posTi", [128, NB], I32)
        posf = sb("posf", [128, NB], F32)
        ang = sb("ang", [128, NB, 8], F32)
        rk = sb("rk", [128, NB, 8], F32)
        rki = sb("rki", [128, NB, 8], I32)
        rw = sb("rw", [128, NB, 8], F32)
        cost = sb("cost", [128, NB, 8], F32)
        sint = sb("sint", [128, NB, 8], F32)
        rgw = sb("rgw", [128, 8, 512], BF16)
        wring = sb("wring", [128, NRING, KC, 512], BF16)
        xs = sb("xs", [128, 2, D], F32)
        ssq = sb("ssq", [128, 8], F32)
        xn = sb("xn", [128, 2, D], BF16)
        hT = sb("hT", [128, 2, KC, TT], BF16)
        cur_hp = [0]
        qtm = sb("qtm", [128, 4, D], BF16)
        ktm = sb("ktm", [128, 2, 512], BF16)
        vtm = sb("vtm", [128, KRING, 256], BF16)
        rt = sb("rt", [128, 4, 128], F32)
        qT = sb("qT", [128, KC, TT], BF16)
        kT = sb("kT", [128, NKV, KRING * 128], BF16)
        PT = sb("PT", [128, 2, 1024], BF16)
        sg = sb("sg", [128, KC, TT], BF16)
        sgr = sb("sgr", [128, KC, TT], BF16)
        yaT = sb("yaT", [128, KC, TT], BF16)
        yrT = sb("yrT", [128, KC, TT], BF16)
        mgT = qT
        lsb = sb("lsb", [128, 2, 256], F32)
        tsb = sb("tsb", [128, 2, 256], F32)
        xrh = sb("xrh", [128, 2, TT + 3], F32)
        xc = sb("xc", [128, 2, 2, TT], F32)
        xcb = sb("xcb", [128, 2, 2, TT], BF16)
        posb = sb("posb", [128, TT], I32)
        rbig = sb("rbig", [128, TT], F32)
        ch_r = sb("ch_r", [128, TT], F32)
        ch_i = sb("ch_i", [128, TT], F32)
        ch_a = sb("ch_a", [128, TT], F32)
        ch_m = sb("ch_m", [128, TT], F32)
        ch_b = sb("ch_b", [128, TT], F32)
        ch_h = sb("ch_h", [128, TT], F32)
        sgate = sb("sgate", [128, 4, TT], F32)
        tgb = sb("tgb", [128, 2, TT], F32)
        t1m = sb("t1m", [128, 4, TT], F32)
        xnew = sb("xnew", [128, 2, D], F32)
        junk2 = sb("junk2", [128, D], BF16)

        POOLS = {"p": [0, 1, 2], "a": [3, 4, 5, 6], "t": [7]}
        pool_ctr = {"p": 0, "od": 0}
        bank_ctr = [0]

        def banks(n=1, pool="p"):
            if pool == "a":
                if n == 2:
                    return 3
                b = 5 + pool_ctr["od"] % 2
                pool_ctr["od"] += 1
                return b
            if pool == "t":
                return 7
            b = POOLS["p"][pool_ctr["p"] % 3]
            pool_ctr["p"] += 1
            return b

        def pkeys(b, n=1):
            return ["ps%d" % (b + i) for i in range(n)]

        P.add("sync", lambda e: e.dma_start(out=pp[:], in_=pp_d[:, :]), w=["pp"], dma="pp")
        P.add("sync", lambda e: e.dma_start(out=cbf[:], in_=cbf_d[:, :]), w=["cbf"], dma="cbf")
        P.add("sync", lambda e: e.dma_start(out=posTi[:], in_=posT_d[:, :]), w=["posTi"], dma="posTi")
        P.add("sync", lambda e: e.dma_start(out=fgbc[:], in_=rows_d[1:2, :].partition_broadcast(128)),
              w=["fgbc"], dma="fgbc")
        crep = t1m[:, 0:2, :].rearrange("p a (k c) -> p (a k) c", c=128)
        bgbc = t1m[:, 2:4, :].rearrange("p a b -> p (a b)")
        gatebc = sgate[:, 0:2, :].rearrange("p a b -> p (a b)")
        P.add("sync", lambda e: e.dma_start(out=bgbc, in_=rows_d[0:1, :].partition_broadcast(128)),
              w=["t1m2", "t1m3"], dma="bgbc")
        P.add("dve", lambda e: e.memset(halo[:], 0.0), w=["halo"])
        P.add("dve", lambda e: e.memset(hstate, 0.0), w=["hstate"])
        P.add("dve", lambda e: e.memset(nhalf, -0.5), w=["nhalf"])

        stg = xs[:, :, :].rearrange("p a (k c) -> p (a k) c", c=256)
        if _PRO >= 3:
            crepb = PT[:, 0, :].rearrange("p (k c) -> p k c", c=128)
            P.add("dve", lambda e: e.tensor_copy(out=crepb, in_=pp[:, PP_CT:PP_CT + 8].unsqueeze(2).broadcast_to([128, 8, 128])),
                  r=["pp"], w=["PT0"])
            for pc in range(6):
                slot = pc % NRING
                src = wada_d[:, pc * 512:(pc + 1) * 512].rearrange("(kc p) c -> p kc c", p=128)
                P.add("pool", lambda e, src=src, slot=slot: e.dma_start(out=wring[:, slot, :, :], in_=src),
                      w=["wr%d" % slot, "cc%d" % (pc % 2)] + (["stgdone"] if pc == 5 else []), dma="wa%d" % slot, c=6.0)

                def mm_mod(e, pc=pc, slot=slot):
                    ins = None
                    for kc in range(KC):
                        ins = e.matmul(ps[:, pc, :], lhsT=crepb[:, kc, :], rhs=wring[:, slot, kc, :],
                                       start=(kc == 0), stop=(kc == KC - 1))
                    return ins
                P.add("pe", mm_mod, r=["wr%d" % slot, "PT0"], w=pkeys(pc))
            bank_ctr[0] = 6
        if _PRO >= 2:
            P.add("pool", lambda e: e.dma_start(out=rgw[:], in_=rgw_d[:, :, :]), r=["stgdone"], w=["rgw", "cc0"], dma="rgw")

            for j in range(16):
                P.add("dve", lambda e, j=j: e.tensor_tensor(out=rt[:, 0, :], in0=ps[:, j // 4, (j % 4) * 128:(j % 4 + 1) * 128],
                                                            in1=ident, op=ALU.mult), r=pkeys(j // 4) + ["cbf"], w=["rt"])
                P.add("dve", lambda e, j=j: e.tensor_reduce(out=modpp[:, j:j + 1], in_=rt[:, 0, :], axis=mybir.AxisListType.X,
                                                            op=ALU.add), r=["rt"], w=["modpp"])
            P.add("dve", lambda e: e.tensor_tensor(out=modpp[:, 0:16], in0=modpp[:, 0:16], in1=pp[:, PP_BADA:PP_BADA + 16],
                                                   op=ALU.add), r=["modpp", "pp"], w=["modpp"])
            P.add("dve", lambda e: e.tensor_scalar(out=tmp8, in0=modpp[:, 8:16], scalar1=1.0, scalar2=None, op0=ALU.add),
                  r=["modpp"], w=["tmp8"])
            P.add("dve", lambda e: e.tensor_tensor(out=gpp[:], in0=tmp8, in1=pp[:, PP_NG:PP_NG + 8], op=ALU.mult),
                  r=["tmp8", "pp"], w=["gpp"])
            P.add("dve", lambda e: e.tensor_tensor(out=gatebc, in0=ps[:, 4:6, :].rearrange("p a b -> p (a b)"), in1=bgbc,
                                                   op=ALU.add),
                  r=pkeys(4, 2) + ["t1m2", "t1m3"], w=["sgate0", "sgate1"])
            P.add("dve", lambda e: e.tensor_scalar(out=gatebc, in0=gatebc, scalar1=0.5, scalar2=None, op0=ALU.mult),
                  r=["sgate0", "sgate1"], w=["sgate0", "sgate1"])
        if _PRO >= 4:
            for hf in range(2):
                for q2 in range(2):
                    c0 = hf * 512 + q2 * 256
                    src = wo_d[:, c0:c0 + 256].rearrange("(kc p) c -> p kc c", p=128)
                    P.add("sync", lambda e, src=src: e.dma_start(out=stg, in_=src), w=["xs0", "xs1"], dma="stg")

                    def sc_wo(e, hf=hf, q2=q2, c0=c0):
                        ins = None
                        for kc in range(KC):
                            ins = e.tensor_tensor(out=wring[:, hf, kc, q2 * 256:(q2 + 1) * 256], in0=stg[:, kc, :],
                                                  in1=gatebc[:, c0:c0 + 256], op=ALU.mult)
                        return ins
                    P.add("dve", sc_wo, r=["xs0", "xs1", "sgate0", "sgate1"], w=["wr%d" % hf])
                P.add("sync", lambda e, hf=hf: e.dma_start(out=scr_d[PC_WO[hf]], in_=wring[:, hf, :, :]),
                      r=["wr%d" % hf], w=["scr%d" % PC_WO[hf]], dma="scr%d" % PC_WO[hf])

        if _PRO >= 5:
            P.add("act", lambda e: e.activation(out=tmp8, in_=pp[:, PP_LAM:PP_LAM + 8], func=AF.Exp, scale=-1.0),
                  r=["pp"], w=["tmp8"])
            P.add("act", lambda e: e.activation(out=tmp8, in_=tmp8, func=AF.Ln, bias=1.0), r=["tmp8"], w=["tmp8"], tbl="A")
            P.add("act", lambda e: e.activation(out=espp, in_=pp[:, PP_SINK:PP_SINK + 8], func=AF.Exp),
                  r=["pp"], w=["espp"])
            P.add("dve", lambda e: e.tensor_scalar(out=cp, in0=tmp8, scalar1=-8.0, scalar2=None, op0=ALU.mult),
                  r=["tmp8"], w=["cp"])
            P.add("dve", lambda e: e.tensor_scalar(out=c2p, in0=tmp8, scalar1=-16.0, scalar2=None, op0=ALU.mult),
                  r=["tmp8"], w=["c2p"])
            P.add("dve", lambda e: e.tensor_scalar(out=nba, in0=pp[:, PP_BA:PP_BA + 8], scalar1=-1.0, scalar2=None, op0=ALU.mult),
                  r=["pp"], w=["nba"])
            P.add("dve", lambda e: e.tensor_scalar(out=nbx, in0=pp[:, PP_BX:PP_BX + 8], scalar1=-1.0, scalar2=None, op0=ALU.mult),
                  r=["pp"], w=["nbx"])

            def RT(fn):
                P.add("dve", fn, r=["posTi", "pp", "ropearg"], w=["ropearg"])
            RT(lambda e: e.tensor_copy(out=posf[:], in_=posTi[:]))
            RT(lambda e: e.tensor_tensor(out=ang[:], in0=posf[:].unsqueeze(2).broadcast_to([128, NB, 8]),
                                         in1=pp[:, PP_INVF:PP_INVF + 8].unsqueeze(1).broadcast_to([128, NB, 8]), op=ALU.mult))
            RT(lambda e: e.tensor_scalar(out=rk[:], in0=ang[:], scalar1=1.0 / TWO_PI, scalar2=None, op0=ALU.mult))
            RT(lambda e: e.tensor_copy(out=rki[:], in_=rk[:]))
            RT(lambda e: e.tensor_copy(out=rk[:], in_=rki[:]))
            RT(lambda e: e.scalar_tensor_tensor(out=ang[:], in0=rk[:], scalar=-C1, in1=ang[:], op0=ALU.mult, op1=ALU.add))
            RT(lambda e: e.scalar_tensor_tensor(out=ang[:], in0=rk[:], scalar=-C2, in1=ang[:], op0=ALU.mult, op1=ALU.add))
            for _ in range(2):
                RT(lambda e: e.tensor_scalar(out=rw[:], in0=ang[:], scalar1=math.pi, scalar2=-TWO_PI, op0=ALU.is_gt, op1=ALU.mult))
                RT(lambda e: e.tensor_tensor(out=ang[:], in0=ang[:], in1=rw[:], op=ALU.add))
                RT(lambda e: e.tensor_scalar(out=rw[:], in0=ang[:], scalar1=-math.pi, scalar2=TWO_PI, op0=ALU.is_lt, op1=ALU.mult))
                RT(lambda e: e.tensor_tensor(out=ang[:], in0=ang[:], in1=rw[:], op=ALU.add))
            RT(lambda e: e.tensor_scalar(out=rk[:], in0=ang[:], scalar1=0.5 * math.pi, scalar2=None, op0=ALU.add))
            RT(lambda e: e.tensor_scalar(out=rw[:], in0=rk[:], scalar1=math.pi, scalar2=-TWO_PI, op0=ALU.is_gt, op1=ALU.mult))
            RT(lambda e: e.tensor_tensor(out=rk[:], in0=rk[:], in1=rw[:], op=ALU.add))
            P.add("act", lambda e: e.activation(out=sint[:], in_=ang[:], func=AF.Sin), r=["ropearg"], w=["sint"], tbl="S")
            P.add("act", lambda e: e.activation(out=cost[:], in_=rk[:], func=AF.Sin), r=["ropearg"], w=["cost"], tbl="S")

        ring_ctr = [0]

        cast_ctr = [1]
        cur_ti = [0]

        def load_piece(pid):
            slot = ring_ctr[0] % NRING
            ring_ctr[0] += 1
            if cur_ti[0] == 0 and pid < 17:
                if pid < 13:
                    src_ap, c0 = win_d, pid * 512
                elif pid < 15:
                    src_ap, c0 = wap_d, (pid - 13) * 512
                else:
                    src_ap, c0 = wrp_d, (pid - 15) * 512
                src = src_ap[:, c0:c0 + 512].rearrange("(kc p) c -> p kc c", p=128)
                ck = "cc%d" % (cast_ctr[0] % 2)
                cast_ctr[0] += 1
                P.add("pool", lambda e: e.dma_start(out=wring[:, slot, :, :], in_=src), r=["stgdone"],
                      w=["wr%d" % slot, ck], dma="wc%d" % slot, c=9.0)
                P.add("sync", lambda e: e.dma_start(out=scr_d[pid], in_=wring[:, slot, :, :]),
                      r=["wr%d" % slot], w=["scr%d" % pid], dma="scr%d" % pid, c=4.0)
                return slot
            P.add("sync", lambda e: e.dma_start(out=wring[:, slot, :, :], in_=scr_d[pid]),
                  r=["scr%d" % pid], w=["wr%d" % slot], dma="wr%d" % slot)
            return slot

        def feat_piece(pid, evac):
            slot = load_piece(pid)
            hp = cur_hp[0]
            for j in range(4):
                b = banks(1)

                def mm(e, b=b, j=j, hp=hp):
                    ins = None
                    for kc in range(KC):
                        ins = e.matmul(ps[:, b, :], lhsT=wring[:, slot, kc, j * 128:(j + 1) * 128], rhs=hT[:, hp, kc, :],
                                       start=(kc == 0), stop=(kc == KC - 1))
                    return ins
                P.add("pe", mm, r=["wr%d" % slot, "hT%d" % hp], w=pkeys(b))
                evac(j, b)

        PA = lambda lv, *a, **k: P.add(*a, **k) if _SUB >= lv else None
        for ti in range(NT):
            t0 = ti * TT
            hp = ti % 2
            cur_hp[0] = hp
            cur_ti[0] = ti
            if _STOP < 1:
                continue
            for bl in range(4):
                gb = ti * 4 + bl
                buf = gb % 2
                PA(1, "sync", lambda e, gb=gb, buf=buf: e.dma_start(out=xs[:, buf, :], in_=x_d[gb * 128:(gb + 1) * 128, :]),
                      w=["xs%d" % buf], dma="xs%d" % buf)
                P.next_c = 1.0
                PA(2, "act", lambda e, buf=buf, bl=bl: e.activation(out=xn[:, buf, :], in_=xs[:, buf, :], func=AF.Square,
                                                                    accum_out=ssq[:, bl:bl + 1]),
                      r=["xs%d" % buf], w=["xn%d" % buf, "ssq%d" % bl])
                P.next_c = 0.25
                PA(3, "pool", lambda e, bl=bl: e.tensor_scalar(out=ssq[:, bl:bl + 1], in0=ssq[:, bl:bl + 1], scalar1=1.0 / D,
                                                               scalar2=EPS, op0=ALU.mult, op1=ALU.add),
                   r=["ssq%d" % bl], w=["ssq%d" % bl])
                P.next_c = 1.7
                PA(4, "pool", lambda e, bl=bl: e.tensor_tensor(out=ssq[:, bl:bl + 1], in0=ssq[:, bl:bl + 1], in1=nhalf, op=ALU.pow),
                   r=["ssq%d" % bl, "nhalf"], w=["ssq%d" % bl])
                P.next_c = 1.2
                PA(5, "dve", lambda e, buf=buf, bl=bl: e.tensor_scalar(out=xn[:, buf, :], in0=xs[:, buf, :],
                                                                       scalar1=ssq[:, bl:bl + 1], scalar2=None, op0=ALU.mult),
                      r=["xs%d" % buf, "ssq%d" % bl], w=["xn%d" % buf])
                b = banks(1, "t")
                pb = ps[:, b, :].bitcast(BF16)

                def tr(e, buf=buf, pb=pb):
                    ins = None
                    for kc in range(KC):
                        ins = e.transpose(out=pb[:, kc * 128:(kc + 1) * 128], in_=xn[:, buf, kc * 128:(kc + 1) * 128],
                                          identity=ident)
                    return ins
                P.next_c = 0.7
                PA(6, "pe", tr, r=["xn%d" % buf, "cbf"], w=pkeys(b))

                def aff(e, pb=pb, bl=bl, hp=hp):
                    ins = None
                    for kc in range(KC):
                        ins = e.activation(out=hT[:, hp, kc, bl * 128:(bl + 1) * 128], in_=pb[:, kc * 128:(kc + 1) * 128],
                                           func=AF.Identity, scale=gpp[:, kc:kc + 1], bias=modpp[:, kc:kc + 1])
                    return ins
                P.next_c = 2.5
                PA(7, "act", aff, r=pkeys(b) + ["gpp", "modpp"], w=["hT%d" % hp])

            if _STOP < 2:
                continue
            for qi, pid in enumerate(PC_QKV):
                slot = load_piece(pid)
                for bl in range(4):
                    gb = ti * 4 + bl
                    b = banks(1)

                    def mm(e, b=b, bl=bl, slot=slot, hp=hp):
                        ins = None
                        for kc in range(KC):
                            ins = e.matmul(ps[:, b, :], lhsT=hT[:, hp, kc, bl * 128:(bl + 1) * 128], rhs=wring[:, slot, kc, :],
                                           start=(kc == 0), stop=(kc == KC - 1))
                        return ins
                    P.add("pe", mm, r=["wr%d" % slot, "hT%d" % hp], w=pkeys(b))
                    qb = bl if qi < 2 else bl % 2
                    nh = 8 if qi < 2 else 4
                    src3 = ps[:, b, 0:nh * 64].rearrange("p (h d) -> p h d", d=64)
                    if qi < 2:
                        dst3 = qtm[:, qb, qi * 512:(qi + 1) * 512].rearrange("p (h d) -> p h d", d=64)
                        dkey = "qtm%d_%d" % (qb, qi)
                    else:
                        dst3 = ktm[:, qb, :].rearrange("p (g u d) -> p g u d", u=2, d=64)[:, :, 0, :]
                        dkey = "ktm%d" % qb
                    cosb = cost[:, gb, :].unsqueeze(1).broadcast_to([128, nh, 8])
                    sinb = sint[:, gb, :].unsqueeze(1).broadcast_to([128, nh, 8])
                    rtv = [rt[:, i, 0:nh * 8].rearrange("p (h d) -> p h d", d=8) for i in range(4)]

                    def rope(e, src3=src3, dst3=dst3, cosb=cosb, sinb=sinb, rtv=rtv):
                        x1 = src3[:, :, 0:8]
                        x2 = src3[:, :, 8:16]
                        e.tensor_tensor(out=rtv[0], in0=x1, in1=cosb, op=ALU.mult)
                        e.tensor_tensor(out=rtv[1], in0=x2, in1=sinb, op=ALU.mult)
                        e.tensor_tensor(out=rtv[2], in0=x2, in1=cosb, op=ALU.mult)
                        return e.tensor_tensor(out=rtv[3], in0=x1, in1=sinb, op=ALU.mult)

                    def rope2(e, dst3=dst3, rtv=rtv):
                        e.tensor_tensor(out=dst3[:, :, 0:8], in0=rtv[0], in1=rtv[1], op=ALU.subtract)
                        return e.tensor_tensor(out=dst3[:, :, 8:16], in0=rtv[2], in1=rtv[3], op=ALU.add)
                    P.add("dve", rope, r=pkeys(b) + ["cost", "sint"], w=["rt"])
                    P.next_c = 0.3
                    P.add("dve", rope2, r=["rt"], w=[dkey + "r"])
                    P.add("act", lambda e, src3=src3, dst3=dst3: e.activation(out=dst3[:, :, 16:64], in_=src3[:, :, 16:64],
                                                                              func=AF.Identity),
                          r=pkeys(b), w=[dkey + "p"])
                    if qi == 2:
                        vs = gb % KRING
                        P.next_c = 0.35
                        P.add("act", lambda e, b=b, vs=vs: e.activation(out=vtm[:, vs, :], in_=ps[:, b, 256:512], func=AF.Identity),
                              r=pkeys(b), w=["vtm%d" % vs])
                        k4 = ktm[:, qb, :].rearrange("p (g u d) -> p g u d", u=2, d=64)
                        P.add("pool", lambda e, k4=k4: e.tensor_copy(out=k4[:, :, 1, :], in_=k4[:, :, 0, :]),
                              r=[dkey + "r", dkey + "p"], w=[dkey + "d"])
                        bk = banks(1)
                        pbk = ps[:, bk, :].bitcast(BF16)

                        def trk(e, qb=qb, pbk=pbk):
                            ins = None
                            for g in range(NKV):
                                ins = e.transpose(out=pbk[:, g * 128:(g + 1) * 128], in_=ktm[:, qb, g * 128:(g + 1) * 128],
                                                  identity=ident)
                            return ins
                        P.next_c = 0.4
                        P.add("pe", trk, r=[dkey + "r", dkey + "p", dkey + "d", "cbf"], w=pkeys(bk))
                        P.add("act", lambda e, pbk=pbk, vs=vs: e.activation(
                            out=kT[:, :, vs * 128:(vs + 1) * 128], in_=pbk[:, 0:512].rearrange("p (g t) -> p g t", t=128),
                            func=AF.Identity), r=pkeys(bk), w=["kT%d" % vs])
                    if qi == 1:
                        bq = banks(1)
                        pbq = ps[:, bq, :].bitcast(BF16)

                        def trq(e, qb=qb, pbq=pbq):
                            ins = None
                            for c in range(KC):
                                ins = e.transpose(out=pbq[:, c * 128:(c + 1) * 128], in_=qtm[:, qb, c * 128:(c + 1) * 128],
                                                  identity=ident)
                            return ins
                        P.next_c = 0.7
                        P.add("pe", trq, r=["qtm%d_0r" % qb, "qtm%d_0p" % qb, "qtm%d_1r" % qb, "qtm%d_1p" % qb, "cbf"],
                              w=pkeys(bq))
                        P.next_c = 1.0
                        P.add("act", lambda e, pbq=pbq, bl=bl: e.activation(
                            out=qT[:, :, bl * 128:(bl + 1) * 128], in_=pbq.rearrange("p (c t) -> p c t", t=128),
                            func=AF.Identity), r=pkeys(bq), w=["qT%d" % bl, "mgT"])

            if _STOP < 3:
                continue
            for hf, pid in enumerate(PC_GA):
                def ev(j, b, hf=hf):
                    c = hf * 4 + j
                    tb_ = c % 2
                    P.add("act", lambda e, b=b, tb_=tb_: e.activation(out=tgb[:, tb_, :], in_=ps[:, b, :], func=AF.Tanh, scale=0.5),
                          r=pkeys(b), w=["tgb%d" % tb_], tbl="B")
                    P.add("dve", lambda e, c=c, b=b, tb_=tb_: e.scalar_tensor_tensor(
                        out=sg[:, c, :], in0=tgb[:, tb_, :], scalar=1.0, in1=ps[:, b, :], op0=ALU.add, op1=ALU.mult),
                        r=pkeys(b) + ["tgb%d" % tb_], w=["sg%d" % c])
                feat_piece(pid, ev)

            if _STOP < 4:
                continue
            for bl in range(4):
                gb = ti * 4 + bl
                kbs = [1] if gb == 0 else [0, 1]
                for g in range(NKV):
                    bs = banks(2, "a")
                    sps = ps[:, bs:bs + 2, :].rearrange("p a b -> p (a b)")
                    pbuf = (gb * NKV + g) % 2
                    c_lo = 256 if gb == 0 else 0

                    def mm_s(e, kbs=kbs, g=g, bl=bl, gb=gb, sps=sps):
                        ins = None
                        for kb in kbs:
                            ks = (gb - 1 + kb) % KRING
                            for half in range(2):
                                pr = slice(half * 64, (half + 1) * 64)
                                ins = e.matmul(sps[:, half * 512 + kb * 256: half * 512 + (kb + 1) * 256],
                                               lhsT=kT[pr, g, ks * 128:(ks + 1) * 128],
                                               rhs=qT[pr, 2 * g:2 * g + 2, bl * 128:(bl + 1) * 128],
                                               start=True, stop=True)
                        return ins
                    kkeys = ["kT%d" % ((gb - 1 + kb) % KRING) for kb in kbs]
                    P.next_c = 0.5
                    P.add("pe", mm_s, r=kkeys + ["qT%d" % bl], w=pkeys(bs, 2))
                    v3 = lambda ap, c_lo=c_lo: ap.rearrange("p (h c) -> p h c", c=512)[:, :, c_lo:512]
                    P.next_c = 1.05
                    P.add("act", lambda e, sps=sps, pbuf=pbuf, v3=v3: e.activation(
                        out=v3(PT[:, pbuf, :]), in_=v3(sps), func=AF.Exp, scale=0.125),
                        r=pkeys(bs, 2), w=["PT%d" % pbuf])
                    P.next_c = 0.65
                    P.add("dve", lambda e, pbuf=pbuf, v3=v3: e.tensor_tensor(
                        out=v3(PT[:, pbuf, :]), in0=v3(PT[:, pbuf, :]), in1=v3(maskb), op=ALU.mult),
                        r=["PT%d" % pbuf, "cbf"], w=["PT%d" % pbuf])
                    bo = banks(1, "a")

                    def mm_pv(e, kbs=kbs, g=g, gb=gb, pbuf=pbuf, bo=bo):
                        ins = None
                        for which in range(2):
                            for ki, kb in enumerate(kbs):
                                vs = (gb - 1 + kb) % KRING
                                for half in range(2):
                                    lhsT = vtm[:, vs, g * 64:(g + 1) * 64] if which == 0 else onesb
                                    ins = e.matmul(ps[half * 64:(half + 1) * 64, bo, which * 256:(which + 1) * 256],
                                                   lhsT=lhsT,
                                                   rhs=PT[:, pbuf, half * 512 + kb * 256: half * 512 + (kb + 1) * 256],
                                                   start=(ki == 0), stop=(ki == len(kbs) - 1))
                        return ins
                    vkeys = ["vtm%d" % ((gb - 1 + kb) % KRING) for kb in kbs]
                    P.next_c = 0.9
                    P.add("pe", mm_pv, r=vkeys + ["PT%d" % pbuf, "cbf"], w=pkeys(bo))
                    lb = (gb * NKV + g) % 2

                    def nrm(e, bo=bo, g=g, lb=lb):
                        ins = None
                        for c2 in range(2):
                            ins = e.activation(out=lsb[:, lb, c2 * 128:(c2 + 1) * 128],
                                               in_=ps[:, bo, 256 + c2 * 128:256 + (c2 + 1) * 128],
                                               func=AF.Ln, bias=espp[:, 2 * g + c2:2 * g + c2 + 1])
                        return ins
                    P.next_c = 0.7
                    P.add("act", nrm, r=pkeys(bo) + ["espp"], w=["lsb%d" % lb], tbl="A")
                    P.next_c = 0.4
                    P.add("act", lambda e, lb=lb: e.activation(out=lsb[:, lb, :], in_=lsb[:, lb, :], func=AF.Exp, scale=-1.0,
                                                                              bias=LN_HALF),
                          r=["lsb%d" % lb], w=["lsb%d" % lb])
                    P.next_c = 0.4
                    P.add("dve", lambda e, bo=bo, lb=lb: e.tensor_tensor(out=tsb[:, lb, :], in0=ps[:, bo, 0:256],
                                                                         in1=lsb[:, lb, :], op=ALU.mult),
                          r=pkeys(bo) + ["lsb%d" % lb], w=["tsb%d" % lb])
                    P.next_c = 0.45
                    P.add("dve", lambda e, lb=lb, g=g, bl=bl: e.tensor_tensor(
                        out=yaT[:, 2 * g:2 * g + 2, bl * 128:(bl + 1) * 128],
                        in0=tsb[:, lb, :].rearrange("p (c t) -> p c t", t=128),
                        in1=sg[:, 2 * g:2 * g + 2, bl * 128:(bl + 1) * 128], op=ALU.mult),
                        r=["tsb%d" % lb, "sg%d" % (2 * g), "sg%d" % (2 * g + 1)], w=["yaT"])

            if _STOP < 5:
                continue
            for hf, pid in enumerate(PC_GR):
                def ev(j, b, hf=hf):
                    c = hf * 4 + j
                    tb_ = c % 2
                    P.add("act", lambda e, b=b, tb_=tb_: e.activation(out=tgb[:, tb_, :], in_=ps[:, b, :], func=AF.Tanh, scale=0.5),
                          r=pkeys(b), w=["tgb%d" % tb_], tbl="B")
                    P.add("dve", lambda e, c=c, b=b, tb_=tb_: e.scalar_tensor_tensor(
                        out=sgr[:, c, :], in0=tgb[:, tb_, :], scalar=1.0, in1=ps[:, b, :], op0=ALU.add, op1=ALU.mult),
                        r=pkeys(b) + ["tgb%d" % tb_], w=["sgr%d" % c])
                feat_piece(pid, ev)

            P.add("sync", lambda e, t0=t0: e.dma_start(out=posb[:], in_=posrow_d[0:1, t0:t0 + TT].partition_broadcast(128)),
                  w=["posb"], dma="posb")
            P.add("dve", lambda e: e.tensor_scalar(out=rbig[:], in0=posb[:], scalar1=0.0, scalar2=1e30,
                                                   op0=ALU.is_equal, op1=ALU.mult), r=["posb"], w=["rbig"])

            if _STOP < 6:
                continue
            for hf, pid in enumerate(PC_XR):
                def ev(j, b, hf=hf, ti=ti):
                    c = hf * 4 + j
                    rb = c // 2
                    o2 = c % 2
                    rbb = rb % 2
                    hb = c % 2
                    cw = lambda k: pp[:, PP_CW + c * 4 + k:PP_CW + c * 4 + k + 1]
                    P.add("act", lambda e: e.activation(out=xrh[:, hb, 3:TT + 3], in_=ps[:, b, :], func=AF.Identity),
                          r=pkeys(b), w=["xrhm%d" % hb])
                    P.next_c = 0.2
                    P.add("pool", lambda e: e.tensor_copy(out=xrh[:, hb, 0:3], in_=halo[:, c, :]),
                          r=["halo%d" % c, "halo"], w=["xrhh%d" % hb])

                    ck = ["xrhm%d" % hb, "xrhh%d" % hb, "pp"] + pkeys(b)
                    P.add("dve", lambda e: e.tensor_scalar(out=ps[:, b, :], in0=ps[:, b, :], scalar1=cw(3),
                                                           scalar2=pp[:, PP_CB + c:PP_CB + c + 1], op0=ALU.mult, op1=ALU.add),
                          r=ck, w=pkeys(b))
                    P.add("dve", lambda e: e.scalar_tensor_tensor(out=ps[:, b, :], in0=xrh[:, hb, 0:TT], scalar=cw(0),
                                                                  in1=ps[:, b, :], op0=ALU.mult, op1=ALU.add), r=ck, w=pkeys(b))
                    P.add("dve", lambda e: e.scalar_tensor_tensor(out=ps[:, b, :], in0=xrh[:, hb, 1:TT + 1], scalar=cw(1),
                                                                  in1=ps[:, b, :], op0=ALU.mult, op1=ALU.add), r=ck, w=pkeys(b))
                    P.add("dve", lambda e: e.scalar_tensor_tensor(out=xc[:, rbb, o2, :], in0=xrh[:, hb, 2:TT + 2], scalar=cw(2),
                                                                  in1=ps[:, b, :], op0=ALU.mult, op1=ALU.add),
                          r=ck, w=pkeys(b) + ["xc%d_%d" % (rbb, o2)])
                    P.next_c = 0.2
                    P.add("pool", lambda e: e.tensor_copy(out=halo[:, c, :], in_=xrh[:, hb, TT:TT + 3]),
                          r=["xrhm%d" % hb], w=["halo%d" % c])
                    P.next_c = 0.9
                    P.add("pool", lambda e: e.tensor_copy(out=xcb[:, rbb, o2, :], in_=xc[:, rbb, o2, :]),
                          r=["xc%d_%d" % (rbb, o2)], w=["xcb%d_%d" % (rbb, o2)])
                    if o2 == 1:
                        for oc in range(2):
                            cc = 2 * rb + oc
                            brs = [banks(1), banks(1)]

                            def mm_g(e, brs=brs, oc=oc):
                                ins = None
                                for ax in range(2):
                                    for kc2 in range(2):
                                        ins = e.matmul(ps[:, brs[ax], :],
                                                       lhsT=rgw[:, ax * 4 + rb, kc2 * 256 + oc * 128: kc2 * 256 + (oc + 1) * 128],
                                                       rhs=xcb[:, rbb, kc2, :], start=(kc2 == 0), stop=(kc2 == 1))
                                return ins
                            P.next_c = 0.9
                            P.add("pe", mm_g, r=["rgw", "xcb%d_0" % rbb, "xcb%d_1" % rbb], w=pkeys(brs[0]) + pkeys(brs[1]))
                            for gi, (chb, kname, nb_) in enumerate(((ch_r, "ch_r", nba), (ch_i, "ch_i", nbx))):
                                P.add("act", lambda e, brs=brs, cc=cc, gi=gi, chb=chb, nb_=nb_: e.activation(
                                    out=chb[:], in_=ps[:, brs[gi], :], func=AF.Exp, scale=-1.0, bias=nb_[:, cc:cc + 1]),
                                    r=pkeys(brs[gi]) + ["nba", "nbx"], w=[kname])
                                P.add("act", lambda e, chb=chb: e.activation(out=chb[:], in_=chb[:], func=AF.Ln, bias=1.0),
                                      r=[kname], w=[kname], tbl="A")
                                P.add("act", lambda e, chb=chb: e.activation(out=chb[:], in_=chb[:], func=AF.Exp, scale=-1.0),
                                      r=[kname], w=[kname])
                            P.next_c = 1.1
                            P.add("dve", lambda e: e.tensor_tensor(out=ch_r[:], in0=ch_r[:], in1=rbig[:], op=ALU.add),
                                  r=["ch_r", "rbig"], w=["ch_r"])
                            P.add("act", lambda e, cc=cc: e.activation(out=ch_a[:], in_=ch_r[:], func=AF.Exp,
                                                                       scale=cp[:, cc:cc + 1]),
                                  r=["ch_r", "cp"], w=["ch_a"])

                            P.add("act", lambda e, cc=cc: e.activation(out=ch_m[:], in_=ch_r[:], func=AF.Exp,
                                                                       scale=c2p[:, cc:cc + 1]), r=["ch_r", "c2p"], w=["ch_m"])
                            P.add("dve", lambda e: e.tensor_scalar(out=ch_m[:], in0=ch_m[:], scalar1=0.99999994, scalar2=None,
                                                                   op0=ALU.min), r=["ch_m"], w=["ch_m"])
                            P.add("act", lambda e: e.activation(out=ch_m[:], in_=ch_m[:], func=AF.Ln, scale=-1.0, bias=1.0),
                                  r=["ch_m"], w=["ch_m"], tbl="A")
                            P.add("act", lambda e: e.activation(out=ch_m[:], in_=ch_m[:], func=AF.Exp, scale=0.5, bias=LN_HALF),
                                  r=["ch_m"], w=["ch_m"])
                            P.next_c = 1.5
                            P.add("pool", lambda e, oc=oc: e.tensor_tensor(out=ch_b[:], in0=ch_i[:], in1=xc[:, rbb, oc, :],
                                                                           op=ALU.mult),
                                  r=["ch_i", "xc%d_%d" % (rbb, oc)], w=["ch_b"])
                            P.next_c = 1.1
                            P.add("dve", lambda e: e.tensor_tensor(out=ch_b[:], in0=ch_b[:], in1=ch_m[:], op=ALU.mult),
                                  r=["ch_b", "ch_m"], w=["ch_b"])

                            P.next_c = 1.1
                            P.add("dve", lambda e, cc=cc: e.tensor_tensor_scan(
                                out=ch_h[:], data0=ch_a[:], data1=ch_b[:], initial=hstate[:, cc:cc + 1], op0=ALU.mult, op1=ALU.add),
                                r=["ch_a", "ch_b", "hstate"], w=["ch_h"])
                            P.next_c = 0.1
                            P.add("dve", lambda e, cc=cc: e.tensor_copy(out=hstate[:, cc:cc + 1], in_=ch_h[:, TT - 1:TT]),
                                  r=["ch_h"], w=["hstate"])
                            P.next_c = 1.1
                            P.add("dve", lambda e, cc=cc: e.tensor_tensor(out=yrT[:, cc, :], in0=ch_h[:], in1=sgr[:, cc, :], op=ALU.mult),
                                  r=["ch_h", "sgr%d" % cc], w=["yrT"])
                feat_piece(pid, ev)

            if _STOP < 7:
                continue
            for hf in range(2):
                def ev_ma(j, b):
                    P.add("act", lambda e: e.activation(out=sgate[:, j, :], in_=ps[:, b, :], func=AF.Tanh, scale=0.5),
                          r=pkeys(b), w=["sgate%d" % j], tbl="B")
                feat_piece(PC_MA[hf], ev_ma)
                slot = load_piece(PC_WAP[hf])
                for j in range(4):
                    b = banks(1)

                    def mm(e, b=b, j=j, slot=slot):
                        ins = None
                        for kc in range(KC):
                            ins = e.matmul(ps[:, b, :], lhsT=wring[:, slot, kc, j * 128:(j + 1) * 128], rhs=yaT[:, kc, :],
                                           start=(kc == 0), stop=(kc == KC - 1))
                        return ins
                    P.add("pe", mm, r=["wr%d" % slot, "yaT"], w=pkeys(b))
                    P.add("dve", lambda e, b=b, j=j: e.scalar_tensor_tensor(out=t1m[:, j, :], in0=sgate[:, j, :], scalar=1.0,
                                                                            in1=ps[:, b, :], op0=ALU.add, op1=ALU.mult),
                          r=pkeys(b) + ["sgate%d" % j], w=["t1m%d" % j])
                feat_piece(PC_MR[hf], ev_ma)
                slot = load_piece(PC_WRP[hf])
                for j in range(4):
                    b = banks(1)
                    oc = hf * 4 + j

                    def mm(e, b=b, j=j, slot=slot):
                        ins = None
                        for kc in range(KC):
                            ins = e.matmul(ps[:, b, :], lhsT=wring[:, slot, kc, j * 128:(j + 1) * 128], rhs=yrT[:, kc, :],
                                           start=(kc == 0), stop=(kc == KC - 1))
                        return ins
                    P.add("pe", mm, r=["wr%d" % slot, "yrT"], w=pkeys(b))

                    P.add("dve", lambda e, b=b, j=j: e.scalar_tensor_tensor(out=sgate[:, j, :], in0=sgate[:, j, :], scalar=1.0,
                                                                            in1=ps[:, b, :], op0=ALU.add, op1=ALU.mult),
                          r=pkeys(b) + ["sgate%d" % j], w=["sgate%d" % j])
                    P.next_c = 1.1
                    P.add("dve", lambda e, j=j, oc=oc: e.tensor_tensor(out=mgT[:, oc, :], in0=sgate[:, j, :], in1=t1m[:, j, :],
                                                                       op=ALU.add),
                          r=["sgate%d" % j, "t1m%d" % j], w=["mgT", "qT0", "qT1", "qT2", "qT3"])

            if _STOP < 8:
                continue
            wo_slots = [load_piece(PC_WO[0]), load_piece(PC_WO[1])]
            for bl in range(4):
                gb = ti * 4 + bl
                buf = gb % 2
                P.add("sync", lambda e, gb=gb, buf=buf: e.dma_start(out=xnew[:, buf, :], in_=x_d[gb * 128:(gb + 1) * 128, :]),
                      w=["xnew%d" % buf], dma="xr%d" % buf)
                obk = []
                for hf in range(2):
                    b = banks(1)
                    obk.append(b)

                    def mm(e, b=b, bl=bl, slot=wo_slots[hf]):
                        ins = None
                        for kc in range(KC):
                            ins = e.matmul(ps[:, b, :], lhsT=mgT[:, kc, bl * 128:(bl + 1) * 128], rhs=wring[:, slot, kc, :],
                                           start=(kc == 0), stop=(kc == KC - 1))
                        return ins
                    P.add("pe", mm, r=["wr%d" % wo_slots[hf], "mgT"], w=pkeys(b))
                b0, b1 = obk

                def resid(e, b0=b0, b1=b1, buf=buf):
                    e.tensor_tensor(out=xnew[:, buf, 0:512], in0=ps[:, b0, :], in1=xnew[:, buf, 0:512], op=ALU.add)
                    return e.tensor_tensor(out=xnew[:, buf, 512:1024], in0=ps[:, b1, :], in1=xnew[:, buf, 512:1024], op=ALU.add)
                P.next_c = 1.2
                P.add("dve", resid, r=pkeys(b0) + pkeys(b1) + ["xnew%d" % buf], w=["xnew%d" % buf])
                sk = 4 + bl
                P.next_c = 1.0
                P.add("act", lambda e, buf=buf, sk=sk: e.activation(out=junk2[:], in_=xnew[:, buf, :], func=AF.Square,
                                                                    accum_out=ssq[:, sk:sk + 1]),
                      r=["xnew%d" % buf], w=["junk2", "ssq%d" % sk])
                P.next_c = 0.25
                P.add("pool", lambda e, sk=sk: e.tensor_scalar(out=ssq[:, sk:sk + 1], in0=ssq[:, sk:sk + 1], scalar1=1.0 / D,
                                                               scalar2=EPS, op0=ALU.mult, op1=ALU.add),
                      r=["ssq%d" % sk], w=["ssq%d" % sk])
                P.next_c = 1.7
                P.add("pool", lambda e, sk=sk: e.tensor_tensor(out=ssq[:, sk:sk + 1], in0=ssq[:, sk:sk + 1], in1=nhalf, op=ALU.pow),
                      r=["ssq%d" % sk, "nhalf"], w=["ssq%d" % sk])
                P.next_c = 2.2
                P.add("dve", lambda e, buf=buf, sk=sk: e.scalar_tensor_tensor(
                    out=xnew[:, buf, :], in0=xnew[:, buf, :], scalar=ssq[:, sk:sk + 1], in1=fgbc[:], op0=ALU.mult, op1=ALU.mult),
                    r=["xnew%d" % buf, "ssq%d" % sk, "fgbc"], w=["xnew%d" % buf])
                P.add("sync", lambda e, gb=gb, buf=buf: e.dma_start(out=y_d[gb * 128:(gb + 1) * 128, :], in_=xnew[:, buf, :]),
                      r=["xnew%d" % buf], w=["ydma%d" % buf, "yout%d" % gb], dma="yst%d" % buf)
        P.add("sync", None, r=["yout%d" % gb for gb in range(NB)] if _STOP >= 8 else [])

        with nc.Block() as block:
            P.emit(nc, block, st)
    return nc


_NC_CACHE = {}


def _host_layout(inputs, b):
    f32 = np.float32
    S = inputs["x"].shape[1]
    NB = S // 128
    pos = np.asarray(inputs["positions"][b]).astype(np.int32)
    pp = np.zeros((128, NPP), f32)
    pp[:, PP_CT:PP_CT + 8] = np.asarray(inputs["c"][b], f32).reshape(8, 128).T
    pp[:, PP_BADA:PP_BADA + 24] = np.asarray(inputs["b_ada"][0], f32).reshape(24, 128).T
    pp[:, PP_NG:PP_NG + 8] = np.asarray(inputs["norm_g"][0], f32).reshape(8, 128).T
    pp[:, PP_CW:PP_CW + 32] = np.asarray(inputs["conv_w"][0], f32).reshape(4, 8, 128).transpose(2, 1, 0).reshape(128, 32)
    pp[:, PP_CB:PP_CB + 8] = np.asarray(inputs["conv_b"][0], f32).reshape(8, 128).T
    pp[:, PP_BA:PP_BA + 8] = np.asarray(inputs["rg_ba"][0], f32).reshape(8, 128).T
    pp[:, PP_BX:PP_BX + 8] = np.asarray(inputs["rg_bx"][0], f32).reshape(8, 128).T
    pp[:, PP_LAM:PP_LAM + 8] = np.asarray(inputs["rg_lambda"][0], f32).reshape(8, 128).T
    sinks = np.asarray(inputs["attn_sinks"][0], f32)
    for c in range(8):
        pp[0:64, PP_SINK + c] = sinks[2 * c]
        pp[64:128, PP_SINK + c] = sinks[2 * c + 1]
    return dict(
        x=np.ascontiguousarray(np.asarray(inputs["x"][b], f32)),
        posT=np.ascontiguousarray(pos.reshape(NB, 128).T),
        posrow=np.ascontiguousarray(pos.reshape(1, S)),
        pp=pp,
    )


def kernel(**inputs):
    f32 = np.float32
    x = np.asarray(inputs["x"])
    B, S, _ = x.shape
    if S not in _NC_CACHE:
        _NC_CACHE[S] = build(S)
    nc = _NC_CACHE[S]
    invf = (np.float32(500000.0) ** (-np.arange(0, 16, 2, dtype=np.float32) / np.float32(16))).astype(f32)
    cbf = np.zeros((128, 128 + 1024 + 64), f32)
    cbf[:, 0:128] = np.eye(128, dtype=f32)
    s_idx = np.arange(128)[:, None]
    q_idx = np.arange(128)[None, :]
    off = (s_idx > q_idx).astype(f32)
    dia = (s_idx <= q_idx).astype(f32)
    half_mask = np.concatenate([off, off, dia, dia], axis=1)
    cbf[:, 128:128 + 512] = half_mask
    cbf[:, 128 + 512:128 + 1024] = half_mask
    cbf[:, 1152:1216] = 1.0
    cbf = cbf.astype(ml_dtypes.bfloat16)
    rows = np.stack([np.asarray(inputs["b_ada"][0], f32)[2048:3072], np.asarray(inputs["final_g"], f32)])
    rgw = np.stack([np.asarray(inputs["rg_wa"][0], f32), np.asarray(inputs["rg_wx"][0], f32)])
    rgw = rgw.reshape(2, 4, 2, 128, 256).transpose(3, 0, 1, 2, 4).reshape(128, 8, 512)
    shared = dict(
        rows=np.ascontiguousarray(rows), cbf=cbf, rgw=np.ascontiguousarray(rgw),
        w_ada=np.ascontiguousarray(np.asarray(inputs["w_ada"][0], f32)),
        w_in=np.ascontiguousarray(np.asarray(inputs["w_in"][0], f32)),
        w_ap=np.ascontiguousarray(np.asarray(inputs["w_attn_proj"][0], f32)),
        w_rp=np.ascontiguousarray(np.asarray(inputs["w_rnn_proj"][0], f32)),
        w_o=np.ascontiguousarray(np.asarray(inputs["w_out"][0], f32)),
    )
    in_maps = []
    for b in range(B):
        m = _host_layout(inputs, b)
        m["pp"][:, PP_INVF:PP_INVF + 8] = invf[None, :]
        m.update(shared)
        in_maps.append(m)
    res = run_bass_kernel_spmd(nc, in_maps, core_ids=list(range(B)))
    out = np.stack([np.asarray(res.results[b]["y"], f32) for b in range(B)], axis=0)
    return out
```

```python
import math
import numpy as np
import ml_dtypes
import concourse.bass as bass
import concourse.mybir as mybir
from concourse.bass_utils import run_bass_kernel_spmd

F32 = mybir.dt.float32
BF16 = mybir.dt.bfloat16
I32 = mybir.dt.int32
AF = mybir.ActivationFunctionType
ALU = mybir.AluOpType

D = 1024
KC = 8
TT = 512
NH = 16
NKV = 4
HD = 64
IN_W = 6656
NPIECE = 19
PC_QKV = [0, 1, 2]
PC_GA = [3, 4]
PC_XR = [5, 6]
PC_GR = [7, 8]
PC_MA = [9, 10]
PC_MR = [11, 12]
PC_WAP = [13, 14]
PC_WRP = [15, 16]
PC_WO = [17, 18]
NRING = 3
KRING = 8
EPS = 1e-6
TWO_PI = 2.0 * math.pi
C1 = 6.28125
C2 = TWO_PI - C1
LN_HALF = math.log(0.5)

PP_CT, PP_BADA, PP_NG, PP_CW, PP_CB, PP_BA, PP_BX, PP_LAM, PP_SINK, PP_INVF = 0, 8, 32, 40, 72, 80, 88, 96, 104, 112
NPP = 120
_STOP = 99
_PRO = 99
_SUB = 99


class Prog:
    LIMIT = 10 ** 9
    LOG = None
    SCHED = True
    DEFC = {"sync": 3.0, "act": 0.62, "dve": 0.6, "pool": 0.9, "pe": 2.05}
    TBL_SWITCH = 2.7
    TBL_BONUS = 8.0
    WINDOW = 1300

    def __init__(self):
        self.ops = []
        self.last_w = {}
        self.readers = {}
        self.real_w = {}
        self.next_c = None

    def add(self, eng, fn, r=(), w=(), dma=None, c=None, tbl=None):
        idx = len(self.ops)
        if idx >= Prog.LIMIT:
            return -1
        if Prog.LOG is not None:
            import sys
            Prog.LOG.append((idx, eng, sys._getframe(1).f_lineno, sys._getframe(2).f_lineno))
        raw = set()
        for k in r:
            if k in self.real_w:
                raw.add(self.real_w[k])
        for k in w:
            self.real_w[k] = idx
        w = list(w) + [k for k in r if k.startswith("ps") and k not in w]
        r = [k for k in r if not k.startswith("ps")]
        deps = set()
        for k in r:
            if k in self.last_w:
                deps.add(self.last_w[k])
        for k in w:
            if k in self.last_w:
                deps.add(self.last_w[k])
            for rd in self.readers.get(k, ()):
                deps.add(rd)
        for k in r:
            self.readers.setdefault(k, []).append(idx)
        for k in w:
            self.last_w[k] = idx
            self.readers[k] = []
        deps.discard(idx)
        raw.discard(idx)
        if c is None and self.next_c is not None:
            c = self.next_c
        self.next_c = None
        self.ops.append(dict(eng=eng, fn=fn, deps=deps, raw=raw, dma=dma, tbl=tbl, c=(Prog.DEFC[eng] if c is None else c)))
        return idx

    def schedule(self):
        ops = self.ops
        n = len(ops)
        if not Prog.SCHED:
            return list(range(n))
        succ = [[] for _ in range(n)]
        ndep = [0] * n
        for i, o in enumerate(ops):
            ndep[i] = len(o["deps"])
            for d in o["deps"]:
                succ[d].append(i)
        engs = ["sync", "act", "dve", "pool", "pe"]
        free_at = {e: 0.0 for e in engs}
        ready = {e: [] for e in engs}
        rtime = [0.0] * n
        fin = [0.0] * n
        start_t = [0.0] * n
        blocker = [-1] * n
        for i in range(n):
            if ndep[i] == 0:
                ready[ops[i]["eng"]].append(i)
        blevel = [0.0] * n
        for i in range(n - 1, -1, -1):
            m = 0.0
            for j in succ[i]:
                if blevel[j] > m:
                    m = blevel[j]
            blevel[i] = m + ops[i]["c"]
        order = []
        WINDOW = Prog.WINDOW
        cur_tbl = [None]
        nxt = 0
        done = [False] * n
        while len(order) < n:
            while nxt < n and done[nxt]:
                nxt += 1
            best = None
            for e in engs:
                cands = [i for i in ready[e] if i <= nxt + WINDOW]
                if not cands:
                    continue
                t_e = max(free_at[e], min(rtime[i] for i in cands))
                pick = None
                for i in cands:
                    if rtime[i] > t_e + 1e-9:
                        continue
                    pr = blevel[i]
                    tb = ops[i]["tbl"]
                    if e == "act" and (tb is None or tb == cur_tbl[0]):
                        pr += Prog.TBL_BONUS
                    if pick is None or pr > pick[0] or (pr == pick[0] and i < pick[1]):
                        pick = (pr, i)
                i = pick[1]
                st = max(rtime[i], free_at[e])
                tb = ops[i]["tbl"]
                if e == "act" and tb is not None and tb != cur_tbl[0]:
                    st += Prog.TBL_SWITCH
                key = (st, i)
                if best is None or key < best[0]:
                    best = (key, e, i)
            if best is None:
                for e in engs:
                    for i in ready[e]:
                        key = (max(rtime[i], free_at[e]), i)
                        if best is None or i < best[2]:
                            best = (key, e, i)
            (st, _), e, i = best
            ready[e].remove(i)
            o = ops[i]
            if e == "act" and o["tbl"] is not None:
                cur_tbl[0] = o["tbl"]
            if o["dma"] is not None:
                free_at[e] = st + 0.15
                fin[i] = st + o["c"]
            else:
                fin[i] = st + o["c"]
                free_at[e] = fin[i]
            done[i] = True
            order.append(i)
            start_t[i] = st
            for j in succ[i]:
                ndep[j] -= 1
                if fin[i] > rtime[j]:
                    blocker[j] = i
                rtime[j] = max(rtime[j], fin[i])
                if ndep[j] == 0:
                    ready[ops[j]["eng"]].append(j)
        self.est_us = max(fin) if n else 0.0
        self.sim = (start_t, fin, blocker, order)
        return order

    def emit(self, nc, block, stack):
        ops = self.ops
        engs = ["sync", "act", "dve", "pool", "pe"]
        order = self.schedule()
        chained = ("act", "dve", "pool")

        def needs_wait(o, p, d):
            if p["dma"] is not None:
                return True
            if p["eng"] != o["eng"]:
                return True
            return p["eng"] in chained

        for o in ops:
            o["sig"] = False
        for o in ops:
            for d in o["deps"]:
                p = ops[d]
                if p["dma"] is None and needs_wait(o, p, d):
                    p["sig"] = True
        esem = {e: stack.enter_context(nc.semaphore("e_" + e)) for e in engs}
        dsem = {}
        dcount = {}
        ecount = {e: 0 for e in engs}
        for i in order:
            o = ops[i]
            if o["dma"] is not None:
                k = o["dma"]
                if k not in dsem:
                    dsem[k] = stack.enter_context(nc.semaphore("d_" + k))
                    dcount[k] = 0
                dcount[k] += 16
                o["sval"] = (dsem[k], dcount[k], "d_" + k)
            elif o["sig"]:
                ecount[o["eng"]] += 1
                o["sval"] = (esem[o["eng"]], ecount[o["eng"]], "e_" + o["eng"])
        per_eng = {e: [] for e in engs}
        for i in order:
            per_eng[ops[i]["eng"]].append(ops[i])

        def run(eng_name, eng):
            waited = {}
            for o in per_eng[eng_name]:
                need = {}
                for d in o["deps"]:
                    p = ops[d]
                    if not needs_wait(o, p, d):
                        continue
                    sem, val, name = p["sval"]
                    if need.get(name, (None, 0))[1] < val:
                        need[name] = (sem, val)
                for name, (sem, val) in need.items():
                    if waited.get(name, 0) < val:
                        eng.wait_ge(sem, val)
                        waited[name] = val
                if o["fn"] is None:
                    continue
                ins = o["fn"](eng)
                if o["dma"] is not None:
                    ins.then_inc(o["sval"][0], 16)
                elif o["sig"]:
                    ins.then_inc(o["sval"][0], 1)

        @block.sync
        def _(e):
            run("sync", e)

        @block.scalar
        def _(e):
            run("act", e)

        @block.vector
        def _(e):
            run("dve", e)

        @block.gpsimd
        def _(e):
            run("pool", e)

        @block.tensor
        def _(e):
            run("pe", e)


def build(S):
    from contextlib import ExitStack
    NB = S // 128
    NT = S // TT
    nc = bass.Bass("TRN2", target_bir_lowering=False)
    dt = nc.dram_tensor
    x_d = dt("x", [S, D], F32, kind="ExternalInput").ap()
    posT_d = dt("posT", [128, NB], I32, kind="ExternalInput").ap()
    posrow_d = dt("posrow", [1, S], I32, kind="ExternalInput").ap()
    pp_d = dt("pp", [128, NPP], F32, kind="ExternalInput").ap()
    rows_d = dt("rows", [2, D], F32, kind="ExternalInput").ap()
    cbf_d = dt("cbf", [128, 128 + 1024 + 64], BF16, kind="ExternalInput").ap()
    wada_d = dt("w_ada", [D, 3 * D], F32, kind="ExternalInput").ap()
    win_d = dt("w_in", [D, IN_W], F32, kind="ExternalInput").ap()
    wap_d = dt("w_ap", [D, D], F32, kind="ExternalInput").ap()
    wrp_d = dt("w_rp", [D, D], F32, kind="ExternalInput").ap()
    wo_d = dt("w_o", [D, D], F32, kind="ExternalInput").ap()
    rgw_d = dt("rgw", [128, 8, 512], F32, kind="ExternalInput").ap()
    scr_d = dt("wscr", [NPIECE, 128, KC, 512], BF16, kind="Internal").ap()
    y_d = dt("y", [S, D], F32, kind="ExternalOutput").ap()

    P = Prog()
    with ExitStack() as st:
        sb = lambda name, shape, dtype: st.enter_context(nc.sbuf_tensor("s_" + name, shape, dtype))
        ps = st.enter_context(nc.psum_tensor("ps", [128, 8, 512], F32))
        pp = sb("pp", [128, NPP], F32)
        cbf = sb("cbf", [128, 128 + 1024 + 64], BF16)
        ident = cbf[:, 0:128]
        maskb = cbf[:, 128:128 + 1024]
        onesb = cbf[:, 1152:1216]
        fgbc = sb("fgbc", [128, D], F32)
        modpp = sb("modpp", [128, 24], F32)
        gpp = sb("gpp", [128, 8], F32)
        small = sb("small", [128, 64], F32)
        cp = small[:, 0:8]
        c2p = small[:, 8:16]
        espp = small[:, 16:24]
        tmp8 = small[:, 24:32]
        hstate = small[:, 32:40]
        nba = small[:, 40:48]
        nbx = small[:, 48:56]
        nhalf = small[:, 56:57]
        halo = sb("halo", [128, 8, 3], F32)
        posTi = sb("posTi", [128, NB], I32)
        posf = sb("posf", [128, NB], F32)
        ang = sb("ang", [128, NB, 8], F32)
        rk = sb("rk", [128, NB, 8], F32)
        rki = sb("rki", [128, NB, 8], I32)
        rw = sb("rw", [128, NB, 8], F32)
        cost = sb("cost", [128, NB, 8], F32)
        sint = sb("sint", [128, NB, 8], F32)
        rgw = sb("rgw", [128, 8, 512], BF16)
        wring = sb("wring", [128, NRING, KC, 512], BF16)
        xs = sb("xs", [128, 2, D], F32)
        ssq = sb("ssq", [128, 8], F32)
        xn = sb("xn", [128, 2, D], BF16)
        hT = sb("hT", [128, 2, KC, TT], BF16)
        cur_hp = [0]
        qtm = sb("qtm", [128, 4, D], BF16)
        ktm = sb("ktm", [128, 2, 512], BF16)
        vtm = sb("vtm", [128, KRING, 256], BF16)
        rt = sb("rt", [128, 4, 128], F32)
        qT = sb("qT", [128, KC, TT], BF16)
        kT = sb("kT", [128, NKV, KRING * 128], BF16)
        PT = sb("PT", [128, 2, 1024], BF16)
        sg = sb("sg", [128, KC, TT], BF16)
        sgr = sb("sgr", [128, KC, TT], BF16)
        yaT = sb("yaT", [128, KC, TT], BF16)
        yrT = sb("yrT", [128, KC, TT], BF16)
        mgT = qT
        lsb = sb("lsb", [128, 2, 256], F32)
        tsb = sb("tsb", [128, 2, 256], F32)
        xrh = sb("xrh", [128, 2, TT + 3], F32)
        xc = sb("xc", [128, 2, 2, TT], F32)
        xcb = sb("xcb", [128, 2, 2, TT], BF16)
        posb = sb("posb", [128, TT], I32)
        rbig = sb("rbig", [128, TT], F32)
        ch_r = sb("ch_r", [128, TT], F32)
        ch_i = sb("ch_i", [128, TT], F32)
        ch_a = sb("ch_a", [128, TT], F32)
        ch_m = sb("ch_m", [128, TT], F32)
        ch_b = sb("ch_b", [128, TT], F32)
        ch_h = sb("ch_h", [128, TT], F32)
        sgate = sb("sgate", [128, 4, TT], F32)
        tgb = sb("tgb", [128, 2, TT], F32)
        t1m = sb("t1m", [128, 4, TT], F32)
        xnew = sb("xnew", [128, 2, D], F32)
        junk2 = sb("junk2", [128, D], BF16)

        POOLS = {"p": [0, 1, 2], "a": [3, 4, 5, 6], "t": [7]}
        pool_ctr = {"p": 0, "od": 0}
        bank_ctr = [0]

        def banks(n=1, pool="p"):
            if pool == "a":
                if n == 2:
                    return 3
                b = 5 + pool_ctr["od"] % 2
                pool_ctr["od"] += 1
                return b
            if pool == "t":
                return 7
            b = POOLS["p"][pool_ctr["p"] % 3]
            pool_ctr["p"] += 1
            return b

        def pkeys(b, n=1):
            return ["ps%d" % (b + i) for i in range(n)]

        P.add("sync", lambda e: e.dma_start(out=pp[:], in_=pp_d[:, :]), w=["pp"], dma="pp")
        P.add("sync", lambda e: e.dma_start(out=cbf[:], in_=cbf_d[:, :]), w=["cbf"], dma="cbf")
        P.add("sync", lambda e: e.dma_start(out=posTi[:], in_=posT_d[:, :]), w=["posTi"], dma="posTi")
        P.add("sync", lambda e: e.dma_start(out=fgbc[:], in_=rows_d[1:2, :].partition_broadcast(128)),
              w=["fgbc"], dma="fgbc")
        crep = t1m[:, 0:2, :].rearrange("p a (k c) -> p (a k) c", c=128)
        bgbc = t1m[:, 2:4, :].rearrange("p a b -> p (a b)")
        gatebc = sgate[:, 0:2, :].rearrange("p a b -> p (a b)")
        P.add("sync", lambda e: e.dma_start(out=bgbc, in_=rows_d[0:1, :].partition_broadcast(128)),
              w=["t1m2", "t1m3"], dma="bgbc")
        P.add("dve", lambda e: e.memset(halo[:], 0.0), w=["halo"])
        P.add("dve", lambda e: e.memset(hstate, 0.0), w=["hstate"])
        P.add("dve", lambda e: e.memset(nhalf, -0.5), w=["nhalf"])

        stg = xs[:, :, :].rearrange("p a (k c) -> p (a k) c", c=256)
        if _PRO >= 3:
            crepb = PT[:, 0, :].rearrange("p (k c) -> p k c", c=128)
            P.add("dve", lambda e: e.tensor_copy(out=crepb, in_=pp[:, PP_CT:PP_CT + 8].unsqueeze(2).broadcast_to([128, 8, 128])),
                  r=["pp"], w=["PT0"])
            for pc in range(6):
                slot = pc % NRING
                src = wada_d[:, pc * 512:(pc + 1) * 512].rearrange("(kc p) c -> p kc c", p=128)
                P.add("pool", lambda e, src=src, slot=slot: e.dma_start(out=wring[:, slot, :, :], in_=src),
                      w=["wr%d" % slot, "cc%d" % (pc % 2)] + (["stgdone"] if pc == 3 else []), dma="wa%d" % slot, c=6.0)

                def mm_mod(e, pc=pc, slot=slot):
                    ins = None
                    for kc in range(KC):
                        ins = e.matmul(ps[:, pc, :], lhsT=crepb[:, kc, :], rhs=wring[:, slot, kc, :],
                                       start=(kc == 0), stop=(kc == KC - 1))
                    return ins
                P.add("pe", mm_mod, r=["wr%d" % slot, "PT0"], w=pkeys(pc))
            bank_ctr[0] = 6
        if _PRO >= 2:
            P.add("pool", lambda e: e.dma_start(out=rgw[:], in_=rgw_d[:, :, :]), r=["stgdone"], w=["rgw", "cc0"], dma="rgw")

            for j in range(16):
                P.add("dve", lambda e, j=j: e.tensor_tensor(out=rt[:, 0, :], in0=ps[:, j // 4, (j % 4) * 128:(j % 4 + 1) * 128],
                                                            in1=ident, op=ALU.mult), r=pkeys(j // 4) + ["cbf"], w=["rt"])
                P.add("dve", lambda e, j=j: e.tensor_reduce(out=modpp[:, j:j + 1], in_=rt[:, 0, :], axis=mybir.AxisListType.X,
                                                            op=ALU.add), r=["rt"], w=["modpp"])
            P.add("dve", lambda e: e.tensor_tensor(out=modpp[:, 0:16], in0=modpp[:, 0:16], in1=pp[:, PP_BADA:PP_BADA + 16],
                                                   op=ALU.add), r=["modpp", "pp"], w=["modpp"])
            P.add("dve", lambda e: e.tensor_scalar(out=tmp8, in0=modpp[:, 8:16], scalar1=1.0, scalar2=None, op0=ALU.add),
                  r=["modpp"], w=["tmp8"])
            P.add("dve", lambda e: e.tensor_tensor(out=gpp[:], in0=tmp8, in1=pp[:, PP_NG:PP_NG + 8], op=ALU.mult),
                  r=["tmp8", "pp"], w=["gpp"])
            P.add("dve", lambda e: e.tensor_tensor(out=gatebc, in0=ps[:, 4:6, :].rearrange("p a b -> p (a b)"), in1=bgbc,
                                                   op=ALU.add),
                  r=pkeys(4, 2) + ["t1m2", "t1m3"], w=["sgate0", "sgate1"])
            P.add("dve", lambda e: e.tensor_scalar(out=gatebc, in0=gatebc, scalar1=0.5, scalar2=None, op0=ALU.mult),
                  r=["sgate0", "sgate1"], w=["sgate0", "sgate1"])
        if _PRO >= 4:
            for hf in range(2):
                for q2 in range(2):
                    c0 = hf * 512 + q2 * 256
                    src = wo_d[:, c0:c0 + 256].rearrange("(kc p) c -> p kc c", p=128)
                    P.add("sync", lambda e, src=src: e.dma_start(out=stg, in_=src), w=["xs0", "xs1"], dma="stg")

                    def sc_wo(e, hf=hf, q2=q2, c0=c0):
                        ins = None
                        for kc in range(KC):
                            ins = e.tensor_tensor(out=wring[:, hf, kc, q2 * 256:(q2 + 1) * 256], in0=stg[:, kc, :],
                                                  in1=gatebc[:, c0:c0 + 256], op=ALU.mult)
                        return ins
                    P.add("dve", sc_wo, r=["xs0", "xs1", "sgate0", "sgate1"], w=["wr%d" % hf])
                P.add("sync", lambda e, hf=hf: e.dma_start(out=scr_d[PC_WO[hf]], in_=wring[:, hf, :, :]),
                      r=["wr%d" % hf], w=["scr%d" % PC_WO[hf]], dma="scr%d" % PC_WO[hf])

        if _PRO >= 5:
            P.add("act", lambda e: e.activation(out=tmp8, in_=pp[:, PP_LAM:PP_LAM + 8], func=AF.Exp, scale=-1.0),
                  r=["pp"], w=["tmp8"])
            P.add("act", lambda e: e.activation(out=tmp8, in_=tmp8, func=AF.Ln, bias=1.0), r=["tmp8"], w=["tmp8"], tbl="A")
            P.add("act", lambda e: e.activation(out=espp, in_=pp[:, PP_SINK:PP_SINK + 8], func=AF.Exp),
                  r=["pp"], w=["espp"])
            P.add("dve", lambda e: e.tensor_scalar(out=cp, in0=tmp8, scalar1=-8.0, scalar2=None, op0=ALU.mult),
                  r=["tmp8"], w=["cp"])
            P.add("dve", lambda e: e.tensor_scalar(out=c2p, in0=tmp8, scalar1=-16.0, scalar2=None, op0=ALU.mult),
                  r=["tmp8"], w=["c2p"])
            P.add("dve", lambda e: e.tensor_scalar(out=nba, in0=pp[:, PP_BA:PP_BA + 8], scalar1=-1.0, scalar2=None, op0=ALU.mult),
                  r=["pp"], w=["nba"])
            P.add("dve", lambda e: e.tensor_scalar(out=nbx, in0=pp[:, PP_BX:PP_BX + 8], scalar1=-1.0, scalar2=None, op0=ALU.mult),
                  r=["pp"], w=["nbx"])

            def RT(fn):
                P.add("dve", fn, r=["posTi", "pp", "ropearg"], w=["ropearg"])
            RT(lambda e: e.tensor_copy(out=posf[:], in_=posTi[:]))
            RT(lambda e: e.tensor_tensor(out=ang[:], in0=posf[:].unsqueeze(2).broadcast_to([128, NB, 8]),
                                         in1=pp[:, PP_INVF:PP_INVF + 8].unsqueeze(1).broadcast_to([128, NB, 8]), op=ALU.mult))
            RT(lambda e: e.tensor_scalar(out=rk[:], in0=ang[:], scalar1=1.0 / TWO_PI, scalar2=None, op0=ALU.mult))
            RT(lambda e: e.tensor_copy(out=rki[:], in_=rk[:]))
            RT(lambda e: e.tensor_copy(out=rk[:], in_=rki[:]))
            RT(lambda e: e.scalar_tensor_tensor(out=ang[:], in0=rk[:], scalar=-C1, in1=ang[:], op0=ALU.mult, op1=ALU.add))
            RT(lambda e: e.scalar_tensor_tensor(out=ang[:], in0=rk[:], scalar=-C2, in1=ang[:], op0=ALU.mult, op1=ALU.add))
            for _ in range(2):
                RT(lambda e: e.tensor_scalar(out=rw[:], in0=ang[:], scalar1=math.pi, scalar2=-TWO_PI, op0=ALU.is_gt, op1=ALU.mult))
                RT(lambda e: e.tensor_tensor(out=ang[:], in0=ang[:], in1=rw[:], op=ALU.add))
                RT(lambda e: e.tensor_scalar(out=rw[:], in0=ang[:], scalar1=-math.pi, scalar2=TWO_PI, op0=ALU.is_lt, op1=ALU.mult))
                RT(lambda e: e.tensor_tensor(out=ang[:], in0=ang[:], in1=rw[:], op=ALU.add))
            RT(lambda e: e.tensor_scalar(out=rk[:], in0=ang[:], scalar1=0.5 * math.pi, scalar2=None, op0=ALU.add))
            RT(lambda e: e.tensor_scalar(out=rw[:], in0=rk[:], scalar1=math.pi, scalar2=-TWO_PI, op0=ALU.is_gt, op1=ALU.mult))
            RT(lambda e: e.tensor_tensor(out=rk[:], in0=rk[:], in1=rw[:], op=ALU.add))
            P.add("act", lambda e: e.activation(out=sint[:], in_=ang[:], func=AF.Sin), r=["ropearg"], w=["sint"], tbl="S")
            P.add("act", lambda e: e.activation(out=cost[:], in_=rk[:], func=AF.Sin), r=["ropearg"], w=["cost"], tbl="S")

        ring_ctr = [0]

        cast_ctr = [1]
        cur_ti = [0]

        def load_piece(pid):
            slot = ring_ctr[0] % NRING
            ring_ctr[0] += 1
            if cur_ti[0] == 0 and pid < 17:
                if pid < 13:
                    src_ap, c0 = win_d, pid * 512
                elif pid < 15:
                    src_ap, c0 = wap_d, (pid - 13) * 512
                else:
                    src_ap, c0 = wrp_d, (pid - 15) * 512
                src = src_ap[:, c0:c0 + 512].rearrange("(kc p) c -> p kc c", p=128)
                ck = "cc%d" % (cast_ctr[0] % 2)
                cast_ctr[0] += 1
                P.add("pool", lambda e: e.dma_start(out=wring[:, slot, :, :], in_=src), r=["stgdone"],
                      w=["wr%d" % slot, ck], dma="wc%d" % slot, c=9.0)
                P.add("sync", lambda e: e.dma_start(out=scr_d[pid], in_=wring[:, slot, :, :]),
                      r=["wr%d" % slot], w=["scr%d" % pid], dma="scr%d" % pid, c=4.0)
                return slot
            P.add("sync", lambda e: e.dma_start(out=wring[:, slot, :, :], in_=scr_d[pid]),
                  r=["scr%d" % pid], w=["wr%d" % slot], dma="wr%d" % slot)
            return slot

        def feat_piece(pid, evac):
            slot = load_piece(pid)
            hp = cur_hp[0]
            for j in range(4):
                b = banks(1)

                def mm(e, b=b, j=j, hp=hp):
                    ins = None
                    for kc in range(KC):
                        ins = e.matmul(ps[:, b, :], lhsT=wring[:, slot, kc, j * 128:(j + 1) * 128], rhs=hT[:, hp, kc, :],
                                       start=(kc == 0), stop=(kc == KC - 1))
                    return ins
                P.add("pe", mm, r=["wr%d" % slot, "hT%d" % hp], w=pkeys(b))
                evac(j, b)

        PA = lambda lv, *a, **k: P.add(*a, **k) if _SUB >= lv else None
        for ti in range(NT):
            t0 = ti * TT
            hp = ti % 2
            cur_hp[0] = hp
            cur_ti[0] = ti
            if _STOP < 1:
                continue
            for bl in range(4):
                gb = ti * 4 + bl
                buf = gb % 2
                PA(1, "sync", lambda e, gb=gb, buf=buf: e.dma_start(out=xs[:, buf, :], in_=x_d[gb * 128:(gb + 1) * 128, :]),
                      w=["xs%d" % buf], dma="xs%d" % buf)
                P.next_c = 1.0
                PA(2, "act", lambda e, buf=buf, bl=bl: e.activation(out=xn[:, buf, :], in_=xs[:, buf, :], func=AF.Square,
                                                                    accum_out=ssq[:, bl:bl + 1]),
                      r=["xs%d" % buf], w=["xn%d" % buf, "ssq%d" % bl])
                P.next_c = 0.25
                PA(3, "pool", lambda e, bl=bl: e.tensor_scalar(out=ssq[:, bl:bl + 1], in0=ssq[:, bl:bl + 1], scalar1=1.0 / D,
                                                               scalar2=EPS, op0=ALU.mult, op1=ALU.add),
                   r=["ssq%d" % bl], w=["ssq%d" % bl])
                P.next_c = 1.7
                PA(4, "pool", lambda e, bl=bl: e.tensor_tensor(out=ssq[:, bl:bl + 1], in0=ssq[:, bl:bl + 1], in1=nhalf, op=ALU.pow),
                   r=["ssq%d" % bl, "nhalf"], w=["ssq%d" % bl])
                P.next_c = 1.2
                PA(5, "dve", lambda e, buf=buf, bl=bl: e.tensor_scalar(out=xn[:, buf, :], in0=xs[:, buf, :],
                                                                       scalar1=ssq[:, bl:bl + 1], scalar2=None, op0=ALU.mult),
                      r=["xs%d" % buf, "ssq%d" % bl], w=["xn%d" % buf])
                b = banks(1, "t")
                pb = ps[:, b, :].bitcast(BF16)

                def tr(e, buf=buf, pb=pb):
                    ins = None
                    for kc in range(KC):
                        ins = e.transpose(out=pb[:, kc * 128:(kc + 1) * 128], in_=xn[:, buf, kc * 128:(kc + 1) * 128],
                                          identity=ident)
                    return ins
                P.next_c = 0.7
                PA(6, "pe", tr, r=["xn%d" % buf, "cbf"], w=pkeys(b))

                def aff(e, pb=pb, bl=bl, hp=hp):
                    ins = None
                    for kc in range(KC):
                        ins = e.activation(out=hT[:, hp, kc, bl * 128:(bl + 1) * 128], in_=pb[:, kc * 128:(kc + 1) * 128],
                                           func=AF.Identity, scale=gpp[:, kc:kc + 1], bias=modpp[:, kc:kc + 1])
                    return ins
                P.next_c = 2.5
                PA(7, "act", aff, r=pkeys(b) + ["gpp", "modpp"], w=["hT%d" % hp])

            if _STOP < 2:
                continue
            for qi, pid in enumerate(PC_QKV):
                slot = load_piece(pid)
                for bl in range(4):
                    gb = ti * 4 + bl
                    b = banks(1)

                    def mm(e, b=b, bl=bl, slot=slot, hp=hp):
                        ins = None
                        for kc in range(KC):
                            ins = e.matmul(ps[:, b, :], lhsT=hT[:, hp, kc, bl * 128:(bl + 1) * 128], rhs=wring[:, slot, kc, :],
                                           start=(kc == 0), stop=(kc == KC - 1))
                        return ins
                    P.add("pe", mm, r=["wr%d" % slot, "hT%d" % hp], w=pkeys(b))
                    qb = bl if qi < 2 else bl % 2
                    nh = 8 if qi < 2 else 4
                    src3 = ps[:, b, 0:nh * 64].rearrange("p (h d) -> p h d", d=64)
                    if qi < 2:
                        dst3 = qtm[:, qb, qi * 512:(qi + 1) * 512].rearrange("p (h d) -> p h d", d=64)
                        dkey = "qtm%d_%d" % (qb, qi)
                    else:
                        dst3 = ktm[:, qb, :].rearrange("p (g u d) -> p g u d", u=2, d=64)[:, :, 0, :]
                        dkey = "ktm%d" % qb
                    cosb = cost[:, gb, :].unsqueeze(1).broadcast_to([128, nh, 8])
                    sinb = sint[:, gb, :].unsqueeze(1).broadcast_to([128, nh, 8])
                    rtv = [rt[:, i, 0:nh * 8].rearrange("p (h d) -> p h d", d=8) for i in range(4)]

                    def rope(e, src3=src3, dst3=dst3, cosb=cosb, sinb=sinb, rtv=rtv):
                        x1 = src3[:, :, 0:8]
                        x2 = src3[:, :, 8:16]
                        e.tensor_tensor(out=rtv[0], in0=x1, in1=cosb, op=ALU.mult)
                        e.tensor_tensor(out=rtv[1], in0=x2, in1=sinb, op=ALU.mult)
                        e.tensor_tensor(out=rtv[2], in0=x2, in1=cosb, op=ALU.mult)
                        return e.tensor_tensor(out=rtv[3], in0=x1, in1=sinb, op=ALU.mult)

                    def rope2(e, dst3=dst3, rtv=rtv):
                        e.tensor_tensor(out=dst3[:, :, 0:8], in0=rtv[0], in1=rtv[1], op=ALU.subtract)
                        return e.tensor_tensor(out=dst3[:, :, 8:16], in0=rtv[2], in1=rtv[3], op=ALU.add)
                    P.add("dve", rope, r=pkeys(b) + ["cost", "sint"], w=["rt"])
                    P.next_c = 0.3
                    P.add("dve", rope2, r=["rt"], w=[dkey + "r"])
                    P.add("act", lambda e, src3=src3, dst3=dst3: e.activation(out=dst3[:, :, 16:64], in_=src3[:, :, 16:64],
                                                                              func=AF.Identity),
                          r=pkeys(b), w=[dkey + "p"])
                    if qi == 2:
                        vs = gb % KRING
                        P.next_c = 0.35
                        P.add("act", lambda e, b=b, vs=vs: e.activation(out=vtm[:, vs, :], in_=ps[:, b, 256:512], func=AF.Identity),
                              r=pkeys(b), w=["vtm%d" % vs])
                        k4 = ktm[:, qb, :].rearrange("p (g u d) -> p g u d", u=2, d=64)
                        P.add("pool", lambda e, k4=k4: e.tensor_copy(out=k4[:, :, 1, :], in_=k4[:, :, 0, :]),
                              r=[dkey + "r", dkey + "p"], w=[dkey + "d"])
                        bk = banks(1)
                        pbk = ps[:, bk, :].bitcast(BF16)

                        def trk(e, qb=qb, pbk=pbk):
                            ins = None
                            for g in range(NKV):
                                ins = e.transpose(out=pbk[:, g * 128:(g + 1) * 128], in_=ktm[:, qb, g * 128:(g + 1) * 128],
                                                  identity=ident)
                            return ins
                        P.next_c = 0.4
                        P.add("pe", trk, r=[dkey + "r", dkey + "p", dkey + "d", "cbf"], w=pkeys(bk))
                        P.add("act", lambda e, pbk=pbk, vs=vs: e.activation(
                            out=kT[:, :, vs * 128:(vs + 1) * 128], in_=pbk[:, 0:512].rearrange("p (g t) -> p g t", t=128),
                            func=AF.Identity), r=pkeys(bk), w=["kT%d" % vs])
                    if qi == 1:
                        bq = banks(1)
                        pbq = ps[:, bq, :].bitcast(BF16)

                        def trq(e, qb=qb, pbq=pbq):
                            ins = None
                            for c in range(KC):
                                ins = e.transpose(out=pbq[:, c * 128:(c + 1) * 128], in_=qtm[:, qb, c * 128:(c + 1) * 128],
                                                  identity=ident)
                            return ins
                        P.next_c = 0.7
                        P.add("pe", trq, r=["qtm%d_0r" % qb, "qtm%d_0p" % qb, "qtm%d_1r" % qb, "qtm%d_1p" % qb, "cbf"],
                              w=pkeys(bq))
                        P.next_c = 1.0
                        P.add("act", lambda e, pbq=pbq, bl=bl: e.activation(
                            out=qT[:, :, bl * 128:(bl + 1) * 128], in_=pbq.rearrange("p (c t) -> p c t", t=128),
                            func=AF.Identity), r=pkeys(bq), w=["qT%d" % bl, "mgT"])

            if _STOP < 3:
                continue
            for hf, pid in enumerate(PC_GA):
                def ev(j, b, hf=hf):
                    c = hf * 4 + j
                    tb_ = c % 2
                    P.add("act", lambda e, b=b, tb_=tb_: e.activation(out=tgb[:, tb_, :], in_=ps[:, b, :], func=AF.Tanh, scale=0.5),
                          r=pkeys(b), w=["tgb%d" % tb_], tbl="B")
                    P.add("dve", lambda e, c=c, b=b, tb_=tb_: e.scalar_tensor_tensor(
                        out=sg[:, c, :], in0=tgb[:, tb_, :], scalar=1.0, in1=ps[:, b, :], op0=ALU.add, op1=ALU.mult),
                        r=pkeys(b) + ["tgb%d" % tb_], w=["sg%d" % c])
                feat_piece(pid, ev)

            if _STOP < 4:
                continue
            for bl in range(4):
                gb = ti * 4 + bl
                kbs = [1] if gb == 0 else [0, 1]
                for g in range(NKV):
                    bs = banks(2, "a")
                    sps = ps[:, bs:bs + 2, :].rearrange("p a b -> p (a b)")
                    pbuf = (gb * NKV + g) % 2
                    c_lo = 256 if gb == 0 else 0

                    def mm_s(e, kbs=kbs, g=g, bl=bl, gb=gb, sps=sps):
                        ins = None
                        for kb in kbs:
                            ks = (gb - 1 + kb) % KRING
                            for half in range(2):
                                pr = slice(half * 64, (half + 1) * 64)
                                ins = e.matmul(sps[:, half * 512 + kb * 256: half * 512 + (kb + 1) * 256],
                                               lhsT=kT[pr, g, ks * 128:(ks + 1) * 128],
                                               rhs=qT[pr, 2 * g:2 * g + 2, bl * 128:(bl + 1) * 128],
                                               start=True, stop=True)
                        return ins
                    kkeys = ["kT%d" % ((gb - 1 + kb) % KRING) for kb in kbs]
                    P.next_c = 0.5
                    P.add("pe", mm_s, r=kkeys + ["qT%d" % bl], w=pkeys(bs, 2))
                    v3 = lambda ap, c_lo=c_lo: ap.rearrange("p (h c) -> p h c", c=512)[:, :, c_lo:512]
                    P.next_c = 1.05
                    P.add("act", lambda e, sps=sps, pbuf=pbuf, v3=v3: e.activation(
                        out=v3(PT[:, pbuf, :]), in_=v3(sps), func=AF.Exp, scale=0.125),
                        r=pkeys(bs, 2), w=["PT%d" % pbuf])
                    P.next_c = 0.65
                    P.add("dve", lambda e, pbuf=pbuf, v3=v3: e.tensor_tensor(
                        out=v3(PT[:, pbuf, :]), in0=v3(PT[:, pbuf, :]), in1=v3(maskb), op=ALU.mult),
                        r=["PT%d" % pbuf, "cbf"], w=["PT%d" % pbuf])
                    bo = banks(1, "a")

                    def mm_pv(e, kbs=kbs, g=g, gb=gb, pbuf=pbuf, bo=bo):
                        ins = None
                        for which in range(2):
                            for ki, kb in enumerate(kbs):
                                vs = (gb - 1 + kb) % KRING
                                for half in range(2):
                                    lhsT = vtm[:, vs, g * 64:(g + 1) * 64] if which == 0 else onesb
                                    ins = e.matmul(ps[half * 64:(half + 1) * 64, bo, which * 256:(which + 1) * 256],
                                                   lhsT=lhsT,
                                                   rhs=PT[:, pbuf, half * 512 + kb * 256: half * 512 + (kb + 1) * 256],
                                                   start=(ki == 0), stop=(ki == len(kbs) - 1))
                        return ins
                    vkeys = ["vtm%d" % ((gb - 1 + kb) % KRING) for kb in kbs]
                    P.next_c = 0.9
                    P.add("pe", mm_pv, r=vkeys + ["PT%d" % pbuf, "cbf"], w=pkeys(bo))
                    lb = (gb * NKV + g) % 2

                    def nrm(e, bo=bo, g=g, lb=lb):
                        ins = None
                        for c2 in range(2):
                            ins = e.activation(out=lsb[:, lb, c2 * 128:(c2 + 1) * 128],
                                               in_=ps[:, bo, 256 + c2 * 128:256 + (c2 + 1) * 128],
                                               func=AF.Ln, bias=espp[:, 2 * g + c2:2 * g + c2 + 1])
                        return ins
                    P.next_c = 0.7
                    P.add("act", nrm, r=pkeys(bo) + ["espp"], w=["lsb%d" % lb], tbl="A")
                    P.next_c = 0.4
                    P.add("act", lambda e, lb=lb: e.activation(out=lsb[:, lb, :], in_=lsb[:, lb, :], func=AF.Exp, scale=-1.0,
                                                                              bias=LN_HALF),
                          r=["lsb%d" % lb], w=["lsb%d" % lb])
                    P.next_c = 0.4
                    P.add("dve", lambda e, bo=bo, lb=lb: e.tensor_tensor(out=tsb[:, lb, :], in0=ps[:, bo, 0:256],
                                                                         in1=lsb[:, lb, :], op=ALU.mult),
                          r=pkeys(bo) + ["lsb%d" % lb], w=["tsb%d" % lb])
                    P.next_c = 0.45
                    P.add("dve", lambda e, lb=lb, g=g, bl=bl: e.tensor_tensor(
                        out=yaT[:, 2 * g:2 * g + 2, bl * 128:(bl + 1) * 128],
                        in0=tsb[:, lb, :].rearrange("p (c t) -> p c t", t=128),
                        in1=sg[:, 2 * g:2 * g + 2, bl * 128:(bl + 1) * 128], op=ALU.mult),
                        r=["tsb%d" % lb, "sg%d" % (2 * g), "sg%d" % (2 * g + 1)], w=["yaT"])

            if _STOP < 5:
                continue
            for hf, pid in enumerate(PC_GR):
                def ev(j, b, hf=hf):
                    c = hf * 4 + j
                    tb_ = c % 2
                    P.add("act", lambda e, b=b, tb_=tb_: e.activation(out=tgb[:, tb_, :], in_=ps[:, b, :], func=AF.Tanh, scale=0.5),
                          r=pkeys(b), w=["tgb%d" % tb_], tbl="B")
                    P.add("dve", lambda e, c=c, b=b, tb_=tb_: e.scalar_tensor_tensor(
                        out=sgr[:, c, :], in0=tgb[:, tb_, :], scalar=1.0, in1=ps[:, b, :], op0=ALU.add, op1=ALU.mult),
                        r=pkeys(b) + ["tgb%d" % tb_], w=["sgr%d" % c])
                feat_piece(pid, ev)

            P.add("sync", lambda e, t0=t0: e.dma_start(out=posb[:], in_=posrow_d[0:1, t0:t0 + TT].partition_broadcast(128)),
                  w=["posb"], dma="posb")
            P.add("dve", lambda e: e.tensor_scalar(out=rbig[:], in0=posb[:], scalar1=0.0, scalar2=1e30,
                                                   op0=ALU.is_equal, op1=ALU.mult), r=["posb"], w=["rbig"])

            if _STOP < 6:
                continue
            for hf, pid in enumerate(PC_XR):
                def ev(j, b, hf=hf, ti=ti):
                    c = hf * 4 + j
                    rb = c // 2
                    o2 = c % 2
                    rbb = rb % 2
                    hb = c % 2
                    cw = lambda k: pp[:, PP_CW + c * 4 + k:PP_CW + c * 4 + k + 1]
                    P.add("act", lambda e: e.activation(out=xrh[:, hb, 3:TT + 3], in_=ps[:, b, :], func=AF.Identity),
                          r=pkeys(b), w=["xrhm%d" % hb])
                    P.next_c = 0.2
                    P.add("pool", lambda e: e.tensor_copy(out=xrh[:, hb, 0:3], in_=halo[:, c, :]),
                          r=["halo%d" % c, "halo"], w=["xrhh%d" % hb])

                    ck = ["xrhm%d" % hb, "xrhh%d" % hb, "pp"] + pkeys(b)
                    P.add("dve", lambda e: e.tensor_scalar(out=ps[:, b, :], in0=ps[:, b, :], scalar1=cw(3),
                                                           scalar2=pp[:, PP_CB + c:PP_CB + c + 1], op0=ALU.mult, op1=ALU.add),
                          r=ck, w=pkeys(b))
                    P.add("dve", lambda e: e.scalar_tensor_tensor(out=ps[:, b, :], in0=xrh[:, hb, 0:TT], scalar=cw(0),
                                                                  in1=ps[:, b, :], op0=ALU.mult, op1=ALU.add), r=ck, w=pkeys(b))
                    P.add("dve", lambda e: e.scalar_tensor_tensor(out=ps[:, b, :], in0=xrh[:, hb, 1:TT + 1], scalar=cw(1),
                                                                  in1=ps[:, b, :], op0=ALU.mult, op1=ALU.add), r=ck, w=pkeys(b))
                    P.add("dve", lambda e: e.scalar_tensor_tensor(out=xc[:, rbb, o2, :], in0=xrh[:, hb, 2:TT + 2], scalar=cw(2),
                                                                  in1=ps[:, b, :], op0=ALU.mult, op1=ALU.add),
                          r=ck, w=pkeys(b) + ["xc%d_%d" % (rbb, o2)])
                    P.next_c = 0.2
                    P.add("pool", lambda e: e.tensor_copy(out=halo[:, c, :], in_=xrh[:, hb, TT:TT + 3]),
                          r=["xrhm%d" % hb], w=["halo%d" % c])
                    P.next_c = 0.9
                    P.add("pool", lambda e: e.tensor_copy(out=xcb[:, rbb, o2, :], in_=xc[:, rbb, o2, :]),
                          r=["xc%d_%d" % (rbb, o2)], w=["xcb%d_%d" % (rbb, o2)])
                    if o2 == 1:
                        for oc in range(2):
                            cc = 2 * rb + oc
                            brs = [banks(1), banks(1)]

                            def mm_g(e, brs=brs, oc=oc):
                                ins = None
                                for ax in range(2):
                                    for kc2 in range(2):
                                        ins = e.matmul(ps[:, brs[ax], :],
                                                       lhsT=rgw[:, ax * 4 + rb, kc2 * 256 + oc * 128: kc2 * 256 + (oc + 1) * 128],
                                                       rhs=xcb[:, rbb, kc2, :], start=(kc2 == 0), stop=(kc2 == 1))
                                return ins
                            P.next_c = 0.9
                            P.add("pe", mm_g, r=["rgw", "xcb%d_0" % rbb, "xcb%d_1" % rbb], w=pkeys(brs[0]) + pkeys(brs[1]))
                            for gi, (chb, kname, nb_) in enumerate(((ch_r, "ch_r", nba), (ch_i, "ch_i", nbx))):
                                P.add("act", lambda e, brs=brs, cc=cc, gi=gi, chb=chb, nb_=nb_: e.activation(
                                    out=chb[:], in_=ps[:, brs[gi], :], func=AF.Exp, scale=-1.0, bias=nb_[:, cc:cc + 1]),
                                    r=pkeys(brs[gi]) + ["nba", "nbx"], w=[kname])
                                P.add("act", lambda e, chb=chb: e.activation(out=chb[:], in_=chb[:], func=AF.Ln, bias=1.0),
                                      r=[kname], w=[kname], tbl="A")
                                P.add("act", lambda e, chb=chb: e.activation(out=chb[:], in_=chb[:], func=AF.Exp, scale=-1.0),
                                      r=[kname], w=[kname])
                            P.next_c = 1.1
                            P.add("dve", lambda e: e.tensor_tensor(out=ch_r[:], in0=ch_r[:], in1=rbig[:], op=ALU.add),
                                  r=["ch_r", "rbig"], w=["ch_r"])
                            P.add("act", lambda e, cc=cc: e.activation(out=ch_a[:], in_=ch_r[:], func=AF.Exp,
                                                                       scale=cp[:, cc:cc + 1]),
                                  r=["ch_r", "cp"], w=["ch_a"])

                            P.add("act", lambda e, cc=cc: e.activation(out=ch_m[:], in_=ch_r[:], func=AF.Exp,
                                                                       scale=c2p[:, cc:cc + 1]), r=["ch_r", "c2p"], w=["ch_m"])
                            P.add("dve", lambda e: e.tensor_scalar(out=ch_m[:], in0=ch_m[:], scalar1=0.99999994, scalar2=None,
                                                                   op0=ALU.min), r=["ch_m"], w=["ch_m"])
                            P.add("act", lambda e: e.activation(out=ch_m[:], in_=ch_m[:], func=AF.Ln, scale=-1.0, bias=1.0),
                                  r=["ch_m"], w=["ch_m"], tbl="A")
                            P.add("act", lambda e: e.activation(out=ch_m[:], in_=ch_m[:], func=AF.Exp, scale=0.5, bias=LN_HALF),
                                  r=["ch_m"], w=["ch_m"])
                            P.next_c = 1.5
                            P.add("pool", lambda e, oc=oc: e.tensor_tensor(out=ch_b[:], in0=ch_i[:], in1=xc[:, rbb, oc, :],
                                                                           op=ALU.mult),
                                  r=["ch_i", "xc%d_%d" % (rbb, oc)], w=["ch_b"])
                            P.next_c = 1.1
                            P.add("dve", lambda e: e.tensor_tensor(out=ch_b[:], in0=ch_b[:], in1=ch_m[:], op=ALU.mult),
                                  r=["ch_b", "ch_m"], w=["ch_b"])

                            P.next_c = 1.1
                            P.add("dve", lambda e, cc=cc: e.tensor_tensor_scan(
                                out=ch_h[:], data0=ch_a[:], data1=ch_b[:], initial=hstate[:, cc:cc + 1], op0=ALU.mult, op1=ALU.add),
                                r=["ch_a", "ch_b", "hstate"], w=["ch_h"])
                            P.next_c = 0.1
                            P.add("dve", lambda e, cc=cc: e.tensor_copy(out=hstate[:, cc:cc + 1], in_=ch_h[:, TT - 1:TT]),
                                  r=["ch_h"], w=["hstate"])
                            P.next_c = 1.1
                            P.add("dve", lambda e, cc=cc: e.tensor_tensor(out=yrT[:, cc, :], in0=ch_h[:], in1=sgr[:, cc, :], op=ALU.mult),
                                  r=["ch_h", "sgr%d" % cc], w=["yrT"])
                feat_piece(pid, ev)

            if _STOP < 7:
                continue
            for hf in range(2):
                def ev_ma(j, b):
                    P.add("act", lambda e: e.activation(out=sgate[:, j, :], in_=ps[:, b, :], func=AF.Tanh, scale=0.5),
                          r=pkeys(b), w=["sgate%d" % j], tbl="B")
                feat_piece(PC_MA[hf], ev_ma)
                slot = load_piece(PC_WAP[hf])
                for j in range(4):
                    b = banks(1)

                    def mm(e, b=b, j=j, slot=slot):
                        ins = None
                        for kc in range(KC):
                            ins = e.matmul(ps[:, b, :], lhsT=wring[:, slot, kc, j * 128:(j + 1) * 128], rhs=yaT[:, kc, :],
                                           start=(kc == 0), stop=(kc == KC - 1))
                        return ins
                    P.add("pe", mm, r=["wr%d" % slot, "yaT"], w=pkeys(b))
                    P.add("dve", lambda e, b=b, j=j: e.scalar_tensor_tensor(out=t1m[:, j, :], in0=sgate[:, j, :], scalar=1.0,
                                                                            in1=ps[:, b, :], op0=ALU.add, op1=ALU.mult),
                          r=pkeys(b) + ["sgate%d" % j], w=["t1m%d" % j])
                feat_piece(PC_MR[hf], ev_ma)
                slot = load_piece(PC_WRP[hf])
                for j in range(4):
                    b = banks(1)
                    oc = hf * 4 + j

                    def mm(e, b=b, j=j, slot=slot):
                        ins = None
                        for kc in range(KC):
                            ins = e.matmul(ps[:, b, :], lhsT=wring[:, slot, kc, j * 128:(j + 1) * 128], rhs=yrT[:, kc, :],
                                           start=(kc == 0), stop=(kc == KC - 1))
                        return ins
                    P.add("pe", mm, r=["wr%d" % slot, "yrT"], w=pkeys(b))

                    P.add("dve", lambda e, b=b, j=j: e.scalar_tensor_tensor(out=sgate[:, j, :], in0=sgate[:, j, :], scalar=1.0,
                                                                            in1=ps[:, b, :], op0=ALU.add, op1=ALU.mult),
                          r=pkeys(b) + ["sgate%d" % j], w=["sgate%d" % j])
                    P.next_c = 1.1
                    P.add("dve", lambda e, j=j, oc=oc: e.tensor_tensor(out=mgT[:, oc, :], in0=sgate[:, j, :], in1=t1m[:, j, :],
                                                                       op=ALU.add),
                          r=["sgate%d" % j, "t1m%d" % j], w=["mgT", "qT0", "qT1", "qT2", "qT3"])

            if _STOP < 8:
                continue
            wo_slots = [load_piece(PC_WO[0]), load_piece(PC_WO[1])]
            for bl in range(4):
                gb = ti * 4 + bl
                buf = gb % 2
                P.add("sync", lambda e, gb=gb, buf=buf: e.dma_start(out=xnew[:, buf, :], in_=x_d[gb * 128:(gb + 1) * 128, :]),
                      w=["xnew%d" % buf], dma="xr%d" % buf)
                obk = []
                for hf in range(2):
                    b = banks(1)
                    obk.append(b)

                    def mm(e, b=b, bl=bl, slot=wo_slots[hf]):
                        ins = None
                        for kc in range(KC):
                            ins = e.matmul(ps[:, b, :], lhsT=mgT[:, kc, bl * 128:(bl + 1) * 128], rhs=wring[:, slot, kc, :],
                                           start=(kc == 0), stop=(kc == KC - 1))
                        return ins
                    P.add("pe", mm, r=["wr%d" % wo_slots[hf], "mgT"], w=pkeys(b))
                b0, b1 = obk

                def resid(e, b0=b0, b1=b1, buf=buf):
                    e.tensor_tensor(out=xnew[:, buf, 0:512], in0=ps[:, b0, :], in1=xnew[:, buf, 0:512], op=ALU.add)
                    return e.tensor_tensor(out=xnew[:, buf, 512:1024], in0=ps[:, b1, :], in1=xnew[:, buf, 512:1024], op=ALU.add)
                P.next_c = 1.2
                P.add("dve", resid, r=pkeys(b0) + pkeys(b1) + ["xnew%d" % buf], w=["xnew%d" % buf])
                sk = 4 + bl
                P.next_c = 1.0
                P.add("act", lambda e, buf=buf, sk=sk: e.activation(out=junk2[:], in_=xnew[:, buf, :], func=AF.Square,
                                                                    accum_out=ssq[:, sk:sk + 1]),
                      r=["xnew%d" % buf], w=["junk2", "ssq%d" % sk])
                P.next_c = 0.25
                P.add("pool", lambda e, sk=sk: e.tensor_scalar(out=ssq[:, sk:sk + 1], in0=ssq[:, sk:sk + 1], scalar1=1.0 / D,
                                                               scalar2=EPS, op0=ALU.mult, op1=ALU.add),
                      r=["ssq%d" % sk], w=["ssq%d" % sk])
                P.next_c = 1.7
                P.add("pool", lambda e, sk=sk: e.tensor_tensor(out=ssq[:, sk:sk + 1], in0=ssq[:, sk:sk + 1], in1=nhalf, op=ALU.pow),
                      r=["ssq%d" % sk, "nhalf"], w=["ssq%d" % sk])
                P.next_c = 2.2
                P.add("dve", lambda e, buf=buf, sk=sk: e.scalar_tensor_tensor(
                    out=xnew[:, buf, :], in0=xnew[:, buf, :], scalar=ssq[:, sk:sk + 1], in1=fgbc[:], op0=ALU.mult, op1=ALU.mult),
                    r=["xnew%d" % buf, "ssq%d" % sk, "fgbc"], w=["xnew%d" % buf])
                P.add("sync", lambda e, gb=gb, buf=buf: e.dma_start(out=y_d[gb * 128:(gb + 1) * 128, :], in_=xnew[:, buf, :]),
                      r=["xnew%d" % buf], w=["ydma%d" % buf, "yout%d" % gb], dma="yst%d" % buf)
        P.add("sync", None, r=["yout%d" % gb for gb in range(NB)] if _STOP >= 8 else [])

        with nc.Block() as block:
            P.emit(nc, block, st)
    return nc


_NC_CACHE = {}


def _host_layout(inputs, b):
    f32 = np.float32
    S = inputs["x"].shape[1]
    NB = S // 128
    pos = np.asarray(inputs["positions"][b]).astype(np.int32)
    pp = np.zeros((128, NPP), f32)
    pp[:, PP_CT:PP_CT + 8] = np.asarray(inputs["c"][b], f32).reshape(8, 128).T
    pp[:, PP_BADA:PP_BADA + 24] = np.asarray(inputs["b_ada"][0], f32).reshape(24, 128).T
    pp[:, PP_NG:PP_NG + 8] = np.asarray(inputs["norm_g"][0], f32).reshape(8, 128).T
    pp[:, PP_CW:PP_CW + 32] = np.asarray(inputs["conv_w"][0], f32).reshape(4, 8, 128).transpose(2, 1, 0).reshape(128, 32)
    pp[:, PP_CB:PP_CB + 8] = np.asarray(inputs["conv_b"][0], f32).reshape(8, 128).T
    pp[:, PP_BA:PP_BA + 8] = np.asarray(inputs["rg_ba"][0], f32).reshape(8, 128).T
    pp[:, PP_BX:PP_BX + 8] = np.asarray(inputs["rg_bx"][0], f32).reshape(8, 128).T
    pp[:, PP_LAM:PP_LAM + 8] = np.asarray(inputs["rg_lambda"][0], f32).reshape(8, 128).T
    sinks = np.asarray(inputs["attn_sinks"][0], f32)
    for c in range(8):
        pp[0:64, PP_SINK + c] = sinks[2 * c]
        pp[64:128, PP_SINK + c] = sinks[2 * c + 1]
    return dict(
        x=np.ascontiguousarray(np.asarray(inputs["x"][b], f32)),
        posT=np.ascontiguousarray(pos.reshape(NB, 128).T),
        posrow=np.ascontiguousarray(pos.reshape(1, S)),
        pp=pp,
    )


def kernel(**inputs):
    f32 = np.float32
    x = np.asarray(inputs["x"])
    B, S, _ = x.shape
    if S not in _NC_CACHE:
        _NC_CACHE[S] = build(S)
    nc = _NC_CACHE[S]
    invf = (np.float32(500000.0) ** (-np.arange(0, 16, 2, dtype=np.float32) / np.float32(16))).astype(f32)
    cbf = np.zeros((128, 128 + 1024 + 64), f32)
    cbf[:, 0:128] = np.eye(128, dtype=f32)
    s_idx = np.arange(128)[:, None]
    q_idx = np.arange(128)[None, :]
    off = (s_idx > q_idx).astype(f32)
    dia = (s_idx <= q_idx).astype(f32)
    half_mask = np.concatenate([off, off, dia, dia], axis=1)
    cbf[:, 128:128 + 512] = half_mask
    cbf[:, 128 + 512:128 + 1024] = half_mask
    cbf[:, 1152:1216] = 1.0
    cbf = cbf.astype(ml_dtypes.bfloat16)
    rows = np.stack([np.asarray(inputs["b_ada"][0], f32)[2048:3072], np.asarray(inputs["final_g"], f32)])
    rgw = np.stack([np.asarray(inputs["rg_wa"][0], f32), np.asarray(inputs["rg_wx"][0], f32)])
    rgw = rgw.reshape(2, 4, 2, 128, 256).transpose(3, 0, 1, 2, 4).reshape(128, 8, 512)
    shared = dict(
        rows=np.ascontiguousarray(rows), cbf=cbf, rgw=np.ascontiguousarray(rgw),
        w_ada=np.ascontiguousarray(np.asarray(inputs["w_ada"][0], f32)),
        w_in=np.ascontiguousarray(np.asarray(inputs["w_in"][0], f32)),
        w_ap=np.ascontiguousarray(np.asarray(inputs["w_attn_proj"][0], f32)),
        w_rp=np.ascontiguousarray(np.asarray(inputs["w_rnn_proj"][0], f32)),
        w_o=np.ascontiguousarray(np.asarray(inputs["w_out"][0], f32)),
    )
    in_maps = []
    for b in range(B):
        m = _host_layout(inputs, b)
        m["pp"][:, PP_INVF:PP_INVF + 8] = invf[None, :]
        m.update(shared)
        in_maps.append(m)
    res = run_bass_kernel_spmd(nc, in_maps, core_ids=list(range(B)))
    out = np.stack([np.asarray(res.results[b]["y"], f32) for b in range(B)], axis=0)
    return out
```

```python
import math
import numpy as np
import ml_dtypes
import concourse.bass as bass
import concourse.mybir as mybir
from concourse.bass_utils import run_bass_kernel_spmd

F32 = mybir.dt.float32
BF16 = mybir.dt.bfloat16
I32 = mybir.dt.int32
AF = mybir.ActivationFunctionType
ALU = mybir.AluOpType

D = 1024
KC = 8
TT = 512
NH = 16
NKV = 4
HD = 64
IN_W = 6656
NPIECE = 19
PC_QKV = [0, 1, 2]
PC_GA = [3, 4]
PC_XR = [5, 6]
PC_GR = [7, 8]
PC_MA = [9, 10]
PC_MR = [11, 12]
PC_WAP = [13, 14]
PC_WRP = [15, 16]
PC_WO = [17, 18]
NRING = 3
KRING = 8
EPS = 1e-6
TWO_PI = 2.0 * math.pi
C1 = 6.28125
C2 = TWO_PI - C1
LN_HALF = math.log(0.5)

PP_CT, PP_BADA, PP_NG, PP_CW, PP_CB, PP_BA, PP_BX, PP_LAM, PP_SINK, PP_INVF = 0, 8, 32, 40, 72, 80, 88, 96, 104, 112
NPP = 120
_STOP = 99
_PRO = 99
_SUB = 99


class Prog:
    LIMIT = 10 ** 9
    LOG = None
    SCHED = True
    DEFC = {"sync": 3.0, "act": 0.62, "dve": 0.6, "pool": 0.9, "pe": 2.05}
    TBL_SWITCH = 2.7
    TBL_BONUS = 2.0
    WINDOW = 1300

    def __init__(self):
        self.ops = []
        self.last_w = {}
        self.readers = {}
        self.real_w = {}
        self.next_c = None

    def add(self, eng, fn, r=(), w=(), dma=None, c=None, tbl=None):
        idx = len(self.ops)
        if idx >= Prog.LIMIT:
            return -1
        if Prog.LOG is not None:
            import sys
            Prog.LOG.append((idx, eng, sys._getframe(1).f_lineno, sys._getframe(2).f_lineno))
        raw = set()
        for k in r:
            if k in self.real_w:
                raw.add(self.real_w[k])
        for k in w:
            self.real_w[k] = idx
        w = list(w) + [k for k in r if k.startswith("ps") and k not in w]
        r = [k for k in r if not k.startswith("ps")]
        deps = set()
        for k in r:
            if k in self.last_w:
                deps.add(self.last_w[k])
        for k in w:
            if k in self.last_w:
                deps.add(self.last_w[k])
            for rd in self.readers.get(k, ()):
                deps.add(rd)
        for k in r:
            self.readers.setdefault(k, []).append(idx)
        for k in w:
            self.last_w[k] = idx
            self.readers[k] = []
        deps.discard(idx)
        raw.discard(idx)
        if c is None and self.next_c is not None:
            c = self.next_c
        self.next_c = None
        self.ops.append(dict(eng=eng, fn=fn, deps=deps, raw=raw, dma=dma, tbl=tbl, c=(Prog.DEFC[eng] if c is None else c)))
        return idx

    def schedule(self):
        ops = self.ops
        n = len(ops)
        if not Prog.SCHED:
            return list(range(n))
        succ = [[] for _ in range(n)]
        ndep = [0] * n
        for i, o in enumerate(ops):
            ndep[i] = len(o["deps"])
            for d in o["deps"]:
                succ[d].append(i)
        engs = ["sync", "act", "dve", "pool", "pe"]
        free_at = {e: 0.0 for e in engs}
        ready = {e: [] for e in engs}
        rtime = [0.0] * n
        fin = [0.0] * n
        start_t = [0.0] * n
        blocker = [-1] * n
        for i in range(n):
            if ndep[i] == 0:
                ready[ops[i]["eng"]].append(i)
        blevel = [0.0] * n
        for i in range(n - 1, -1, -1):
            m = 0.0
            for j in succ[i]:
                if blevel[j] > m:
                    m = blevel[j]
            blevel[i] = m + ops[i]["c"]
        order = []
        WINDOW = Prog.WINDOW
        cur_tbl = [None]
        nxt = 0
        done = [False] * n
        while len(order) < n:
            while nxt < n and done[nxt]:
                nxt += 1
            best = None
            for e in engs:
                cands = [i for i in ready[e] if i <= nxt + WINDOW]
                if not cands:
                    continue
                t_e = max(free_at[e], min(rtime[i] for i in cands))
                pick = None
                for i in cands:
                    if rtime[i] > t_e + 1e-9:
                        continue
                    pr = blevel[i]
                    tb = ops[i]["tbl"]
                    if e == "act" and (tb is None or tb == cur_tbl[0]):
                        pr += Prog.TBL_BONUS
                    if pick is None or pr > pick[0] or (pr == pick[0] and i < pick[1]):
                        pick = (pr, i)
                i = pick[1]
                st = max(rtime[i], free_at[e])
                tb = ops[i]["tbl"]
                if e == "act" and tb is not None and tb != cur_tbl[0]:
                    st += Prog.TBL_SWITCH
                key = (st, i)
                if best is None or key < best[0]:
                    best = (key, e, i)
            if best is None:
                for e in engs:
                    for i in ready[e]:
                        key = (max(rtime[i], free_at[e]), i)
                        if best is None or i < best[2]:
                            best = (key, e, i)
            (st, _), e, i = best
            ready[e].remove(i)
            o = ops[i]
            if e == "act" and o["tbl"] is not None:
                cur_tbl[0] = o["tbl"]
            if o["dma"] is not None:
                free_at[e] = st + 0.15
                fin[i] = st + o["c"]
            else:
                fin[i] = st + o["c"]
                free_at[e] = fin[i]
            done[i] = True
            order.append(i)
            start_t[i] = st
            for j in succ[i]:
                ndep[j] -= 1
                if fin[i] > rtime[j]:
                    blocker[j] = i
                rtime[j] = max(rtime[j], fin[i])
                if ndep[j] == 0:
                    ready[ops[j]["eng"]].append(j)
        self.est_us = max(fin) if n else 0.0
        self.sim = (start_t, fin, blocker, order)
        return order

    def emit(self, nc, block, stack):
        ops = self.ops
        engs = ["sync", "act", "dve", "pool", "pe"]
        order = self.schedule()
        chained = ("act", "dve", "pool")

        def needs_wait(o, p, d):
            if p["dma"] is not None:
                return True
            if p["eng"] != o["eng"]:
                return True
            return p["eng"] in chained

        for o in ops:
            o["sig"] = False
        for o in ops:
            for d in o["deps"]:
                p = ops[d]
                if p["dma"] is None and needs_wait(o, p, d):
                    p["sig"] = True
        esem = {e: stack.enter_context(nc.semaphore("e_" + e)) for e in engs}
        dsem = {}
        dcount = {}
        ecount = {e: 0 for e in engs}
        for i in order:
            o = ops[i]
            if o["dma"] is not None:
                k = o["dma"]
                if k not in dsem:
                    dsem[k] = stack.enter_context(nc.semaphore("d_" + k))
                    dcount[k] = 0
                dcount[k] += 16
                o["sval"] = (dsem[k], dcount[k], "d_" + k)
            elif o["sig"]:
                ecount[o["eng"]] += 1
                o["sval"] = (esem[o["eng"]], ecount[o["eng"]], "e_" + o["eng"])
        per_eng = {e: [] for e in engs}
        for i in order:
            per_eng[ops[i]["eng"]].append(ops[i])

        def run(eng_name, eng):
            waited = {}
            for o in per_eng[eng_name]:
                need = {}
                for d in o["deps"]:
                    p = ops[d]
                    if not needs_wait(o, p, d):
                        continue
                    sem, val, name = p["sval"]
                    if need.get(name, (None, 0))[1] < val:
                        need[name] = (sem, val)
                for name, (sem, val) in need.items():
                    if waited.get(name, 0) < val:
                        eng.wait_ge(sem, val)
                        waited[name] = val
                if o["fn"] is None:
                    continue
                ins = o["fn"](eng)
                if o["dma"] is not None:
                    ins.then_inc(o["sval"][0], 16)
                elif o["sig"]:
                    ins.then_inc(o["sval"][0], 1)

        @block.sync
        def _(e):
            run("sync", e)

        @block.scalar
        def _(e):
            run("act", e)

        @block.vector
        def _(e):
            run("dve", e)

        @block.gpsimd
        def _(e):
            run("pool", e)

        @block.tensor
        def _(e):
            run("pe", e)


def build(S):
    from contextlib import ExitStack
    NB = S // 128
    NT = S // TT
    nc = bass.Bass("TRN2", target_bir_lowering=False)
    dt = nc.dram_tensor
    x_d = dt("x", [S, D], F32, kind="ExternalInput").ap()
    posT_d = dt("posT", [128, NB], I32, kind="ExternalInput").ap()
    posrow_d = dt("posrow", [1, S], I32, kind="ExternalInput").ap()
    pp_d = dt("pp", [128, NPP], F32, kind="ExternalInput").ap()
    rows_d = dt("rows", [2, D], F32, kind="ExternalInput").ap()
    cbf_d = dt("cbf", [128, 128 + 1024 + 64], BF16, kind="ExternalInput").ap()
    wada_d = dt("w_ada", [D, 3 * D], F32, kind="ExternalInput").ap()
    win_d = dt("w_in", [D, IN_W], F32, kind="ExternalInput").ap()
    wap_d = dt("w_ap", [D, D], F32, kind="ExternalInput").ap()
    wrp_d = dt("w_rp", [D, D], F32, kind="ExternalInput").ap()
    wo_d = dt("w_o", [D, D], F32, kind="ExternalInput").ap()
    rgw_d = dt("rgw", [128, 8, 512], F32, kind="ExternalInput").ap()
    scr_d = dt("wscr", [NPIECE, 128, KC, 512], BF16, kind="Internal").ap()
    y_d = dt("y", [S, D], F32, kind="ExternalOutput").ap()

    P = Prog()
    with ExitStack() as st:
        sb = lambda name, shape, dtype: st.enter_context(nc.sbuf_tensor("s_" + name, shape, dtype))
        ps = st.enter_context(nc.psum_tensor("ps", [128, 8, 512], F32))
        pp = sb("pp", [128, NPP], F32)
        cbf = sb("cbf", [128, 128 + 1024 + 64], BF16)
        ident = cbf[:, 0:128]
        maskb = cbf[:, 128:128 + 1024]
        onesb = cbf[:, 1152:1216]
        fgbc = sb("fgbc", [128, D], F32)
        modpp = sb("modpp", [128, 24], F32)
        gpp = sb("gpp", [128, 8], F32)
        small = sb("small", [128, 64], F32)
        cp = small[:, 0:8]
        c2p = small[:, 8:16]
        espp = small[:, 16:24]
        tmp8 = small[:, 24:32]
        hstate = small[:, 32:40]
        nba = small[:, 40:48]
        nbx = small[:, 48:56]
        nhalf = small[:, 56:57]
        halo = sb("halo", [128, 8, 3], F32)
        posTi = sb("posTi", [128, NB], I32)
        posf = sb("posf", [128, NB], F32)
        ang = sb("ang", [128, NB, 8], F32)
        rk = sb("rk", [128, NB, 8], F32)
        rki = sb("rki", [128, NB, 8], I32)
        rw = sb("rw", [128, NB, 8], F32)
        cost = sb("cost", [128, NB, 8], F32)
        sint = sb("sint", [128, NB, 8], F32)
        rgw = sb("rgw", [128, 8, 512], BF16)
        wring = sb("wring", [128, NRING, KC, 512], BF16)
        xs = sb("xs", [128, 2, D], F32)
        ssq = sb("ssq", [128, 8], F32)
        xn = sb("xn", [128, 2, D], BF16)
        hT = sb("hT", [128, 2, KC, TT], BF16)
        cur_hp = [0]
        qtm = sb("qtm", [128, 4, D], BF16)
        ktm = sb("ktm", [128, 2, 512], BF16)
        vtm = sb("vtm", [128, KRING, 256], BF16)
        rt = sb("rt", [128, 4, 128], F32)
        qT = sb("qT", [128, KC, TT], BF16)
        kT = sb("kT", [128, NKV, KRING * 128], BF16)
        PT = sb("PT", [128, 2, 1024], BF16)
        sg = sb("sg", [128, KC, TT], BF16)
        sgr = sb("sgr", [128, KC, TT], BF16)
        yaT = sb("yaT", [128, KC, TT], BF16)
        yrT = sb("yrT", [128, KC, TT], BF16)
        mgT = qT
        lsb = sb("lsb", [128, 2, 256], F32)
        tsb = sb("tsb", [128, 2, 256], F32)
        xrh = sb("xrh", [128, 2, TT + 3], F32)
        xc = sb("xc", [128, 2, 2, TT], F32)
        xcb = sb("xcb", [128, 2, 2, TT], BF16)
        posb = sb("posb", [128, TT], I32)
        rbig = sb("rbig", [128, TT], F32)
        ch_r = sb("ch_r", [128, TT], F32)
        ch_i = sb("ch_i", [128, TT], F32)
        ch_a = sb("ch_a", [128, TT], F32)
        ch_m = sb("ch_m", [128, TT], F32)
        ch_b = sb("ch_b", [128, TT], F32)
        ch_h = sb("ch_h", [128, TT], F32)
        sgate = sb("sgate", [128, 4, TT], F32)
        tgb = sb("tgb", [128, 2, TT], F32)
        t1m = sb("t1m", [128, 4, TT], F32)
        xnew = sb("xnew", [128, 2, D], F32)
        junk2 = sb("junk2", [128, D], BF16)

        POOLS = {"p": [0, 1, 2], "a": [3, 4, 5, 6], "t": [7]}
        pool_ctr = {"p": 0, "od": 0}
        bank_ctr = [0]

        def banks(n=1, pool="p"):
            if pool == "a":
                if n == 2:
                    return 3
                b = 5 + pool_ctr["od"] % 2
                pool_ctr["od"] += 1
                return b
            if pool == "t":
                return 7
            b = POOLS["p"][pool_ctr["p"] % 3]
            pool_ctr["p"] += 1
            return b

        def pkeys(b, n=1):
            return ["ps%d" % (b + i) for i in range(n)]

        P.add("sync", lambda e: e.dma_start(out=pp[:], in_=pp_d[:, :]), w=["pp"], dma="pp")
        P.add("sync", lambda e: e.dma_start(out=cbf[:], in_=cbf_d[:, :]), w=["cbf"], dma="cbf")
        P.add("sync", lambda e: e.dma_start(out=posTi[:], in_=posT_d[:, :]), w=["posTi"], dma="posTi")
        P.add("sync", lambda e: e.dma_start(out=fgbc[:], in_=rows_d[1:2, :].partition_broadcast(128)),
              w=["fgbc"], dma="fgbc")
        crep = t1m[:, 0:2, :].rearrange("p a (k c) -> p (a k) c", c=128)
        bgbc = t1m[:, 2:4, :].rearrange("p a b -> p (a b)")
        gatebc = sgate[:, 0:2, :].rearrange("p a b -> p (a b)")
        P.add("sync", lambda e: e.dma_start(out=bgbc, in_=rows_d[0:1, :].partition_broadcast(128)),
              w=["t1m2", "t1m3"], dma="bgbc")
        P.add("dve", lambda e: e.memset(halo[:], 0.0), w=["halo"])
        P.add("dve", lambda e: e.memset(hstate, 0.0), w=["hstate"])
        P.add("dve", lambda e: e.memset(nhalf, -0.5), w=["nhalf"])

        stg = xs[:, :, :].rearrange("p a (k c) -> p (a k) c", c=256)
        if _PRO >= 3:
            crepb = PT[:, 0, :].rearrange("p (k c) -> p k c", c=128)
            P.add("dve", lambda e: e.tensor_copy(out=crepb, in_=pp[:, PP_CT:PP_CT + 8].unsqueeze(2).broadcast_to([128, 8, 128])),
                  r=["pp"], w=["PT0"])
            for pc in range(6):
                slot = pc % NRING
                src = wada_d[:, pc * 512:(pc + 1) * 512].rearrange("(kc p) c -> p kc c", p=128)
                P.add("pool", lambda e, src=src, slot=slot: e.dma_start(out=wring[:, slot, :, :], in_=src),
                      w=["wr%d" % slot, "cc%d" % (pc % 2)] + (["stgdone"] if pc == 5 else []), dma="wa%d" % slot, c=6.0)

                def mm_mod(e, pc=pc, slot=slot):
                    ins = None
                    for kc in range(KC):
                        ins = e.matmul(ps[:, pc, :], lhsT=crepb[:, kc, :], rhs=wring[:, slot, kc, :],
                                       start=(kc == 0), stop=(kc == KC - 1))
                    return ins
                P.add("pe", mm_mod, r=["wr%d" % slot, "PT0"], w=pkeys(pc))
            bank_ctr[0] = 6
        if _PRO >= 2:
            P.add("pool", lambda e: e.dma_start(out=rgw[:], in_=rgw_d[:, :, :]), r=["stgdone"], w=["rgw", "cc0"], dma="rgw")

            for j in range(16):
                P.add("dve", lambda e, j=j: e.tensor_tensor(out=rt[:, 0, :], in0=ps[:, j // 4, (j % 4) * 128:(j % 4 + 1) * 128],
                                                            in1=ident, op=ALU.mult), r=pkeys(j // 4) + ["cbf"], w=["rt"])
                P.add("dve", lambda e, j=j: e.tensor_reduce(out=modpp[:, j:j + 1], in_=rt[:, 0, :], axis=mybir.AxisListType.X,
                                                            op=ALU.add), r=["rt"], w=["modpp"])
            P.add("dve", lambda e: e.tensor_tensor(out=modpp[:, 0:16], in0=modpp[:, 0:16], in1=pp[:, PP_BADA:PP_BADA + 16],
                                                   op=ALU.add), r=["modpp", "pp"], w=["modpp"])
            P.add("dve", lambda e: e.tensor_scalar(out=tmp8, in0=modpp[:, 8:16], scalar1=1.0, scalar2=None, op0=ALU.add),
                  r=["modpp"], w=["tmp8"])
            P.add("dve", lambda e: e.tensor_tensor(out=gpp[:], in0=tmp8, in1=pp[:, PP_NG:PP_NG + 8], op=ALU.mult),
                  r=["tmp8", "pp"], w=["gpp"])
            P.add("dve", lambda e: e.tensor_tensor(out=gatebc, in0=ps[:, 4:6, :].rearrange("p a b -> p (a b)"), in1=bgbc,
                                                   op=ALU.add),
                  r=pkeys(4, 2) + ["t1m2", "t1m3"], w=["sgate0", "sgate1"])
            P.add("dve", lambda e: e.tensor_scalar(out=gatebc, in0=gatebc, scalar1=0.5, scalar2=None, op0=ALU.mult),
                  r=["sgate0", "sgate1"], w=["sgate0", "sgate1"])
        if _PRO >= 4:
            for hf in range(2):
                for q2 in range(2):
                    c0 = hf * 512 + q2 * 256
                    src = wo_d[:, c0:c0 + 256].rearrange("(kc p) c -> p kc c", p=128)
                    P.add("sync", lambda e, src=src: e.dma_start(out=stg, in_=src), w=["xs0", "xs1"], dma="stg")

                    def sc_wo(e, hf=hf, q2=q2, c0=c0):
                        ins = None
                        for kc in range(KC):
                            ins = e.tensor_tensor(out=wring[:, hf, kc, q2 * 256:(q2 + 1) * 256], in0=stg[:, kc, :],
                                                  in1=gatebc[:, c0:c0 + 256], op=ALU.mult)
                        return ins
                    P.add("dve", sc_wo, r=["xs0", "xs1", "sgate0", "sgate1"], w=["wr%d" % hf])
                P.add("sync", lambda e, hf=hf: e.dma_start(out=scr_d[PC_WO[hf]], in_=wring[:, hf, :, :]),
                      r=["wr%d" % hf], w=["scr%d" % PC_WO[hf]], dma="scr%d" % PC_WO[hf])

        if _PRO >= 5:
            P.add("act", lambda e: e.activation(out=tmp8, in_=pp[:, PP_LAM:PP_LAM + 8], func=AF.Exp, scale=-1.0),
                  r=["pp"], w=["tmp8"])
            P.add("act", lambda e: e.activation(out=tmp8, in_=tmp8, func=AF.Ln, bias=1.0), r=["tmp8"], w=["tmp8"], tbl="A")
            P.add("act", lambda e: e.activation(out=espp, in_=pp[:, PP_SINK:PP_SINK + 8], func=AF.Exp),
                  r=["pp"], w=["espp"])
            P.add("dve", lambda e: e.tensor_scalar(out=cp, in0=tmp8, scalar1=-8.0, scalar2=None, op0=ALU.mult),
                  r=["tmp8"], w=["cp"])
            P.add("dve", lambda e: e.tensor_scalar(out=c2p, in0=tmp8, scalar1=-16.0, scalar2=None, op0=ALU.mult),
                  r=["tmp8"], w=["c2p"])
            P.add("dve", lambda e: e.tensor_scalar(out=nba, in0=pp[:, PP_BA:PP_BA + 8], scalar1=-1.0, scalar2=None, op0=ALU.mult),
                  r=["pp"], w=["nba"])
            P.add("dve", lambda e: e.tensor_scalar(out=nbx, in0=pp[:, PP_BX:PP_BX + 8], scalar1=-1.0, scalar2=None, op0=ALU.mult),
                  r=["pp"], w=["nbx"])

            def RT(fn):
                P.add("dve", fn, r=["posTi", "pp", "ropearg"], w=["ropearg"])
            RT(lambda e: e.tensor_copy(out=posf[:], in_=posTi[:]))
            RT(lambda e: e.tensor_tensor(out=ang[:], in0=posf[:].unsqueeze(2).broadcast_to([128, NB, 8]),
                                         in1=pp[:, PP_INVF:PP_INVF + 8].unsqueeze(1).broadcast_to([128, NB, 8]), op=ALU.mult))
            RT(lambda e: e.tensor_scalar(out=rk[:], in0=ang[:], scalar1=1.0 / TWO_PI, scalar2=None, op0=ALU.mult))
            RT(lambda e: e.tensor_copy(out=rki[:], in_=rk[:]))
            RT(lambda e: e.tensor_copy(out=rk[:], in_=rki[:]))
            RT(lambda e: e.scalar_tensor_tensor(out=ang[:], in0=rk[:], scalar=-C1, in1=ang[:], op0=ALU.mult, op1=ALU.add))
            RT(lambda e: e.scalar_tensor_tensor(out=ang[:], in0=rk[:], scalar=-C2, in1=ang[:], op0=ALU.mult, op1=ALU.add))
            for _ in range(2):
                RT(lambda e: e.tensor_scalar(out=rw[:], in0=ang[:], scalar1=math.pi, scalar2=-TWO_PI, op0=ALU.is_gt, op1=ALU.mult))
                RT(lambda e: e.tensor_tensor(out=ang[:], in0=ang[:], in1=rw[:], op=ALU.add))
                RT(lambda e: e.tensor_scalar(out=rw[:], in0=ang[:], scalar1=-math.pi, scalar2=TWO_PI, op0=ALU.is_lt, op1=ALU.mult))
                RT(lambda e: e.tensor_tensor(out=ang[:], in0=ang[:], in1=rw[:], op=ALU.add))
            RT(lambda e: e.tensor_scalar(out=rk[:], in0=ang[:], scalar1=0.5 * math.pi, scalar2=None, op0=ALU.add))
            RT(lambda e: e.tensor_scalar(out=rw[:], in0=rk[:], scalar1=math.pi, scalar2=-TWO_PI, op0=ALU.is_gt, op1=ALU.mult))
            RT(lambda e: e.tensor_tensor(out=rk[:], in0=rk[:], in1=rw[:], op=ALU.add))
            P.add("act", lambda e: e.activation(out=sint[:], in_=ang[:], func=AF.Sin), r=["ropearg"], w=["sint"], tbl="S")
            P.add("act", lambda e: e.activation(out=cost[:], in_=rk[:], func=AF.Sin), r=["ropearg"], w=["cost"], tbl="S")

        ring_ctr = [0]

        cast_ctr = [1]
        cur_ti = [0]

        def load_piece(pid):
            slot = ring_ctr[0] % NRING
            ring_ctr[0] += 1
            if cur_ti[0] == 0 and pid < 17:
                if pid < 13:
                    src_ap, c0 = win_d, pid * 512
                elif pid < 15:
                    src_ap, c0 = wap_d, (pid - 13) * 512
                else:
                    src_ap, c0 = wrp_d, (pid - 15) * 512
                src = src_ap[:, c0:c0 + 512].rearrange("(kc p) c -> p kc c", p=128)
                ck = "cc%d" % (cast_ctr[0] % 2)
                cast_ctr[0] += 1
                P.add("pool", lambda e: e.dma_start(out=wring[:, slot, :, :], in_=src), r=["stgdone"],
                      w=["wr%d" % slot, ck], dma="wc%d" % slot, c=9.0)
                P.add("sync", lambda e: e.dma_start(out=scr_d[pid], in_=wring[:, slot, :, :]),
                      r=["wr%d" % slot], w=["scr%d" % pid], dma="scr%d" % pid, c=4.0)
                return slot
            P.add("sync", lambda e: e.dma_start(out=wring[:, slot, :, :], in_=scr_d[pid]),
                  r=["scr%d" % pid], w=["wr%d" % slot], dma="wr%d" % slot)
            return slot

        def feat_piece(pid, evac):
            slot = load_piece(pid)
            hp = cur_hp[0]
            for j in range(4):
                b = banks(1)

                def mm(e, b=b, j=j, hp=hp):
                    ins = None
                    for kc in range(KC):
                        ins = e.matmul(ps[:, b, :], lhsT=wring[:, slot, kc, j * 128:(j + 1) * 128], rhs=hT[:, hp, kc, :],
                                       start=(kc == 0), stop=(kc == KC - 1))
                    return ins
                P.add("pe", mm, r=["wr%d" % slot, "hT%d" % hp], w=pkeys(b))
                evac(j, b)

        PA = lambda lv, *a, **k: P.add(*a, **k) if _SUB >= lv else None
        for ti in range(NT):
            t0 = ti * TT
            hp = ti % 2
            cur_hp[0] = hp
            cur_ti[0] = ti
            if _STOP < 1:
                continue
            for bl in range(4):
                gb = ti * 4 + bl
                buf = gb % 2
                PA(1, "sync", lambda e, gb=gb, buf=buf: e.dma_start(out=xs[:, buf, :], in_=x_d[gb * 128:(gb + 1) * 128, :]),
                      w=["xs%d" % buf], dma="xs%d" % buf)
                P.next_c = 1.0
                PA(2, "act", lambda e, buf=buf, bl=bl: e.activation(out=xn[:, buf, :], in_=xs[:, buf, :], func=AF.Square,
                                                                    accum_out=ssq[:, bl:bl + 1]),
                      r=["xs%d" % buf], w=["xn%d" % buf, "ssq%d" % bl])
                P.next_c = 0.25
                PA(3, "pool", lambda e, bl=bl: e.tensor_scalar(out=ssq[:, bl:bl + 1], in0=ssq[:, bl:bl + 1], scalar1=1.0 / D,
                                                               scalar2=EPS, op0=ALU.mult, op1=ALU.add),
                   r=["ssq%d" % bl], w=["ssq%d" % bl])
                P.next_c = 1.7
                PA(4, "pool", lambda e, bl=bl: e.tensor_tensor(out=ssq[:, bl:bl + 1], in0=ssq[:, bl:bl + 1], in1=nhalf, op=ALU.pow),
                   r=["ssq%d" % bl, "nhalf"], w=["ssq%d" % bl])
                P.next_c = 1.2
                PA(5, "dve", lambda e, buf=buf, bl=bl: e.tensor_scalar(out=xn[:, buf, :], in0=xs[:, buf, :],
                                                                       scalar1=ssq[:, bl:bl + 1], scalar2=None, op0=ALU.mult),
                      r=["xs%d" % buf, "ssq%d" % bl], w=["xn%d" % buf])
                b = banks(1, "t")
                pb = ps[:, b, :].bitcast(BF16)

                def tr(e, buf=buf, pb=pb):
                    ins = None
                    for kc in range(KC):
                        ins = e.transpose(out=pb[:, kc * 128:(kc + 1) * 128], in_=xn[:, buf, kc * 128:(kc + 1) * 128],
                                          identity=ident)
                    return ins
                P.next_c = 0.7
                PA(6, "pe", tr, r=["xn%d" % buf, "cbf"], w=pkeys(b))

                def aff(e, pb=pb, bl=bl, hp=hp):
                    ins = None
                    for kc in range(KC):
                        ins = e.activation(out=hT[:, hp, kc, bl * 128:(bl + 1) * 128], in_=pb[:, kc * 128:(kc + 1) * 128],
                                           func=AF.Identity, scale=gpp[:, kc:kc + 1], bias=modpp[:, kc:kc + 1])
                    return ins
                P.next_c = 2.5
                PA(7, "act", aff, r=pkeys(b) + ["gpp", "modpp"], w=["hT%d" % hp])

            if _STOP < 2:
                continue
            for qi, pid in enumerate(PC_QKV):
                slot = load_piece(pid)
                for bl in range(4):
                    gb = ti * 4 + bl
                    b = banks(1)

                    def mm(e, b=b, bl=bl, slot=slot, hp=hp):
                        ins = None
                        for kc in range(KC):
                            ins = e.matmul(ps[:, b, :], lhsT=hT[:, hp, kc, bl * 128:(bl + 1) * 128], rhs=wring[:, slot, kc, :],
                                           start=(kc == 0), stop=(kc == KC - 1))
                        return ins
                    P.add("pe", mm, r=["wr%d" % slot, "hT%d" % hp], w=pkeys(b))
                    qb = bl if qi < 2 else bl % 2
                    nh = 8 if qi < 2 else 4
                    src3 = ps[:, b, 0:nh * 64].rearrange("p (h d) -> p h d", d=64)
                    if qi < 2:
                        dst3 = qtm[:, qb, qi * 512:(qi + 1) * 512].rearrange("p (h d) -> p h d", d=64)
                        dkey = "qtm%d_%d" % (qb, qi)
                    else:
                        dst3 = ktm[:, qb, :].rearrange("p (g u d) -> p g u d", u=2, d=64)[:, :, 0, :]
                        dkey = "ktm%d" % qb
                    cosb = cost[:, gb, :].unsqueeze(1).broadcast_to([128, nh, 8])
                    sinb = sint[:, gb, :].unsqueeze(1).broadcast_to([128, nh, 8])
                    rtv = [rt[:, i, 0:nh * 8].rearrange("p (h d) -> p h d", d=8) for i in range(4)]

                    def rope(e, src3=src3, dst3=dst3, cosb=cosb, sinb=sinb, rtv=rtv):
                        x1 = src3[:, :, 0:8]
                        x2 = src3[:, :, 8:16]
                        e.tensor_tensor(out=rtv[0], in0=x1, in1=cosb, op=ALU.mult)
                        e.tensor_tensor(out=rtv[1], in0=x2, in1=sinb, op=ALU.mult)
                        e.tensor_tensor(out=rtv[2], in0=x2, in1=cosb, op=ALU.mult)
                        return e.tensor_tensor(out=rtv[3], in0=x1, in1=sinb, op=ALU.mult)

                    def rope2(e, dst3=dst3, rtv=rtv):
                        e.tensor_tensor(out=dst3[:, :, 0:8], in0=rtv[0], in1=rtv[1], op=ALU.subtract)
                        return e.tensor_tensor(out=dst3[:, :, 8:16], in0=rtv[2], in1=rtv[3], op=ALU.add)
                    P.add("dve", rope, r=pkeys(b) + ["cost", "sint"], w=["rt"])
                    P.next_c = 0.3
                    P.add("dve", rope2, r=["rt"], w=[dkey + "r"])
                    P.add("act", lambda e, src3=src3, dst3=dst3: e.activation(out=dst3[:, :, 16:64], in_=src3[:, :, 16:64],
                                                                              func=AF.Identity),
                          r=pkeys(b), w=[dkey + "p"])
                    if qi == 2:
                        vs = gb % KRING
                        P.next_c = 0.35
                        P.add("act", lambda e, b=b, vs=vs: e.activation(out=vtm[:, vs, :], in_=ps[:, b, 256:512], func=AF.Identity),
                              r=pkeys(b), w=["vtm%d" % vs])
                        k4 = ktm[:, qb, :].rearrange("p (g u d) -> p g u d", u=2, d=64)
                        P.add("pool", lambda e, k4=k4: e.tensor_copy(out=k4[:, :, 1, :], in_=k4[:, :, 0, :]),
                              r=[dkey + "r", dkey + "p"], w=[dkey + "d"])
                        bk = banks(1)
                        pbk = ps[:, bk, :].bitcast(BF16)

                        def trk(e, qb=qb, pbk=pbk):
                            ins = None
                            for g in range(NKV):
                                ins = e.transpose(out=pbk[:, g * 128:(g + 1) * 128], in_=ktm[:, qb, g * 128:(g + 1) * 128],
                                                  identity=ident)
                            return ins
                        P.next_c = 0.4
                        P.add("pe", trk, r=[dkey + "r", dkey + "p", dkey + "d", "cbf"], w=pkeys(bk))
                        P.add("act", lambda e, pbk=pbk, vs=vs: e.activation(
                            out=kT[:, :, vs * 128:(vs + 1) * 128], in_=pbk[:, 0:512].rearrange("p (g t) -> p g t", t=128),
                            func=AF.Identity), r=pkeys(bk), w=["kT%d" % vs])
                    if qi == 1:
                        bq = banks(1)
                        pbq = ps[:, bq, :].bitcast(BF16)

                        def trq(e, qb=qb, pbq=pbq):
                            ins = None
                            for c in range(KC):
                                ins = e.transpose(out=pbq[:, c * 128:(c + 1) * 128], in_=qtm[:, qb, c * 128:(c + 1) * 128],
                                                  identity=ident)
                            return ins
                        P.next_c = 0.7
                        P.add("pe", trq, r=["qtm%d_0r" % qb, "qtm%d_0p" % qb, "qtm%d_1r" % qb, "qtm%d_1p" % qb, "cbf"],
                              w=pkeys(bq))
                        P.next_c = 1.0
                        P.add("act", lambda e, pbq=pbq, bl=bl: e.activation(
                            out=qT[:, :, bl * 128:(bl + 1) * 128], in_=pbq.rearrange("p (c t) -> p c t", t=128),
                            func=AF.Identity), r=pkeys(bq), w=["qT%d" % bl, "mgT"])

            if _STOP < 3:
                continue
            for hf, pid in enumerate(PC_GA):
                def ev(j, b, hf=hf):
                    c = hf * 4 + j
                    tb_ = c % 2
                    P.add("act", lambda e, b=b, tb_=tb_: e.activation(out=tgb[:, tb_, :], in_=ps[:, b, :], func=AF.Tanh, scale=0.5),
                          r=pkeys(b), w=["tgb%d" % tb_], tbl="B")
                    P.add("dve", lambda e, c=c, b=b, tb_=tb_: e.scalar_tensor_tensor(
                        out=sg[:, c, :], in0=tgb[:, tb_, :], scalar=1.0, in1=ps[:, b, :], op0=ALU.add, op1=ALU.mult),
                        r=pkeys(b) + ["tgb%d" % tb_], w=["sg%d" % c])
                feat_piece(pid, ev)

            if _STOP < 4:
                continue
            for bl in range(4):
                gb = ti * 4 + bl
                kbs = [1] if gb == 0 else [0, 1]
                for g in range(NKV):
                    bs = banks(2, "a")
                    sps = ps[:, bs:bs + 2, :].rearrange("p a b -> p (a b)")
                    pbuf = (gb * NKV + g) % 2
                    c_lo = 256 if gb == 0 else 0

                    def mm_s(e, kbs=kbs, g=g, bl=bl, gb=gb, sps=sps):
                        ins = None
                        for kb in kbs:
                            ks = (gb - 1 + kb) % KRING
                            for half in range(2):
                                pr = slice(half * 64, (half + 1) * 64)
                                ins = e.matmul(sps[:, half * 512 + kb * 256: half * 512 + (kb + 1) * 256],
                                               lhsT=kT[pr, g, ks * 128:(ks + 1) * 128],
                                               rhs=qT[pr, 2 * g:2 * g + 2, bl * 128:(bl + 1) * 128],
                                               start=True, stop=True)
                        return ins
                    kkeys = ["kT%d" % ((gb - 1 + kb) % KRING) for kb in kbs]
                    P.next_c = 0.5
                    P.add("pe", mm_s, r=kkeys + ["qT%d" % bl], w=pkeys(bs, 2))
                    v3 = lambda ap, c_lo=c_lo: ap.rearrange("p (h c) -> p h c", c=512)[:, :, c_lo:512]
                    P.next_c = 1.05
                    P.add("act", lambda e, sps=sps, pbuf=pbuf, v3=v3: e.activation(
                        out=v3(PT[:, pbuf, :]), in_=v3(sps), func=AF.Exp, scale=0.125),
                        r=pkeys(bs, 2), w=["PT%d" % pbuf])
                    P.next_c = 0.65
                    P.add("dve", lambda e, pbuf=pbuf, v3=v3: e.tensor_tensor(
                        out=v3(PT[:, pbuf, :]), in0=v3(PT[:, pbuf, :]), in1=v3(maskb), op=ALU.mult),
                        r=["PT%d" % pbuf, "cbf"], w=["PT%d" % pbuf])
                    bo = banks(1, "a")

                    def mm_pv(e, kbs=kbs, g=g, gb=gb, pbuf=pbuf, bo=bo):
                        ins = None
                        for which in range(2):
                            for ki, kb in enumerate(kbs):
                                vs = (gb - 1 + kb) % KRING
                                for half in range(2):
                                    lhsT = vtm[:, vs, g * 64:(g + 1) * 64] if which == 0 else onesb
                                    ins = e.matmul(ps[half * 64:(half + 1) * 64, bo, which * 256:(which + 1) * 256],
                                                   lhsT=lhsT,
                                                   rhs=PT[:, pbuf, half * 512 + kb * 256: half * 512 + (kb + 1) * 256],
                                                   start=(ki == 0), stop=(ki == len(kbs) - 1))
                        return ins
                    vkeys = ["vtm%d" % ((gb - 1 + kb) % KRING) for kb in kbs]
                    P.next_c = 0.9
                    P.add("pe", mm_pv, r=vkeys + ["PT%d" % pbuf, "cbf"], w=pkeys(bo))
                    lb = (gb * NKV + g) % 2

                    def nrm(e, bo=bo, g=g, lb=lb):
                        ins = None
                        for c2 in range(2):
                            ins = e.activation(out=lsb[:, lb, c2 * 128:(c2 + 1) * 128],
                                               in_=ps[:, bo, 256 + c2 * 128:256 + (c2 + 1) * 128],
                                               func=AF.Ln, bias=espp[:, 2 * g + c2:2 * g + c2 + 1])
                        return ins
                    P.next_c = 0.7
                    P.add("act", nrm, r=pkeys(bo) + ["espp"], w=["lsb%d" % lb], tbl="A")
                    P.next_c = 0.4
                    P.add("act", lambda e, lb=lb: e.activation(out=lsb[:, lb, :], in_=lsb[:, lb, :], func=AF.Exp, scale=-1.0,
                                                                              bias=LN_HALF),
                          r=["lsb%d" % lb], w=["lsb%d" % lb])
                    P.next_c = 0.4
                    P.add("dve", lambda e, bo=bo, lb=lb: e.tensor_tensor(out=tsb[:, lb, :], in0=ps[:, bo, 0:256],
                                                                         in1=lsb[:, lb, :], op=ALU.mult),
                          r=pkeys(bo) + ["lsb%d" % lb], w=["tsb%d" % lb])
                    P.next_c = 0.45
                    P.add("dve", lambda e, lb=lb, g=g, bl=bl: e.tensor_tensor(
                        out=yaT[:, 2 * g:2 * g + 2, bl * 128:(bl + 1) * 128],
                        in0=tsb[:, lb, :].rearrange("p (c t) -> p c t", t=128),
                        in1=sg[:, 2 * g:2 * g + 2, bl * 128:(bl + 1) * 128], op=ALU.mult),
                        r=["tsb%d" % lb, "sg%d" % (2 * g), "sg%d" % (2 * g + 1)], w=["yaT"])

            if _STOP < 5:
                continue
            for hf, pid in enumerate(PC_GR):
                def ev(j, b, hf=hf):
                    c = hf * 4 + j
                    tb_ = c % 2
                    P.add("act", lambda e, b=b, tb_=tb_: e.activation(out=tgb[:, tb_, :], in_=ps[:, b, :], func=AF.Tanh, scale=0.5),
                          r=pkeys(b), w=["tgb%d" % tb_], tbl="B")
                    P.add("dve", lambda e, c=c, b=b, tb_=tb_: e.scalar_tensor_tensor(
                        out=sgr[:, c, :], in0=tgb[:, tb_, :], scalar=1.0, in1=ps[:, b, :], op0=ALU.add, op1=ALU.mult),
                        r=pkeys(b) + ["tgb%d" % tb_], w=["sgr%d" % c])
                feat_piece(pid, ev)

            P.add("sync", lambda e, t0=t0: e.dma_start(out=posb[:], in_=posrow_d[0:1, t0:t0 + TT].partition_broadcast(128)),
                  w=["posb"], dma="posb")
            P.add("dve", lambda e: e.tensor_scalar(out=rbig[:], in0=posb[:], scalar1=0.0, scalar2=1e30,
                                                   op0=ALU.is_equal, op1=ALU.mult), r=["posb"], w=["rbig"])

            if _STOP < 6:
                continue
            for hf, pid in enumerate(PC_XR):
                def ev(j, b, hf=hf, ti=ti):
                    c = hf * 4 + j
                    rb = c // 2
                    o2 = c % 2
                    rbb = rb % 2
                    hb = c % 2
                    cw = lambda k: pp[:, PP_CW + c * 4 + k:PP_CW + c * 4 + k + 1]
                    P.add("act", lambda e: e.activation(out=xrh[:, hb, 3:TT + 3], in_=ps[:, b, :], func=AF.Identity),
                          r=pkeys(b), w=["xrhm%d" % hb])
                    P.next_c = 0.2
                    P.add("pool", lambda e: e.tensor_copy(out=xrh[:, hb, 0:3], in_=halo[:, c, :]),
                          r=["halo%d" % c, "halo"], w=["xrhh%d" % hb])

                    ck = ["xrhm%d" % hb, "xrhh%d" % hb, "pp"] + pkeys(b)
                    P.add("dve", lambda e: e.tensor_scalar(out=ps[:, b, :], in0=ps[:, b, :], scalar1=cw(3),
                                                           scalar2=pp[:, PP_CB + c:PP_CB + c + 1], op0=ALU.mult, op1=ALU.add),
                          r=ck, w=pkeys(b))
                    P.add("dve", lambda e: e.scalar_tensor_tensor(out=ps[:, b, :], in0=xrh[:, hb, 0:TT], scalar=cw(0),
                                                                  in1=ps[:, b, :], op0=ALU.mult, op1=ALU.add), r=ck, w=pkeys(b))
                    P.add("dve", lambda e: e.scalar_tensor_tensor(out=ps[:, b, :], in0=xrh[:, hb, 1:TT + 1], scalar=cw(1),
                                                                  in1=ps[:, b, :], op0=ALU.mult, op1=ALU.add), r=ck, w=pkeys(b))
                    P.add("dve", lambda e: e.scalar_tensor_tensor(out=xc[:, rbb, o2, :], in0=xrh[:, hb, 2:TT + 2], scalar=cw(2),
                                                                  in1=ps[:, b, :], op0=ALU.mult, op1=ALU.add),
                          r=ck, w=pkeys(b) + ["xc%d_%d" % (rbb, o2)])
                    P.next_c = 0.2
                    P.add("pool", lambda e: e.tensor_copy(out=halo[:, c, :], in_=xrh[:, hb, TT:TT + 3]),
                          r=["xrhm%d" % hb], w=["halo%d" % c])
                    P.next_c = 0.9
                    P.add("pool", lambda e: e.tensor_copy(out=xcb[:, rbb, o2, :], in_=xc[:, rbb, o2, :]),
                          r=["xc%d_%d" % (rbb, o2)], w=["xcb%d_%d" % (rbb, o2)])
                    if o2 == 1:
                        for oc in range(2):
                            cc = 2 * rb + oc
                            brs = [banks(1), banks(1)]

                            def mm_g(e, brs=brs, oc=oc):
                                ins = None
                                for ax in range(2):
                                    for kc2 in range(2):
                                        ins = e.matmul(ps[:, brs[ax], :],
                                                       lhsT=rgw[:, ax * 4 + rb, kc2 * 256 + oc * 128: kc2 * 256 + (oc + 1) * 128],
                                                       rhs=xcb[:, rbb, kc2, :], start=(kc2 == 0), stop=(kc2 == 1))
                                return ins
                            P.next_c = 0.9
                            P.add("pe", mm_g, r=["rgw", "xcb%d_0" % rbb, "xcb%d_1" % rbb], w=pkeys(brs[0]) + pkeys(brs[1]))
                            for gi, (chb, kname, nb_) in enumerate(((ch_r, "ch_r", nba), (ch_i, "ch_i", nbx))):
                                P.add("act", lambda e, brs=brs, cc=cc, gi=gi, chb=chb, nb_=nb_: e.activation(
                                    out=chb[:], in_=ps[:, brs[gi], :], func=AF.Exp, scale=-1.0, bias=nb_[:, cc:cc + 1]),
                                    r=pkeys(brs[gi]) + ["nba", "nbx"], w=[kname])
                                P.add("act", lambda e, chb=chb: e.activation(out=chb[:], in_=chb[:], func=AF.Ln, bias=1.0),
                                      r=[kname], w=[kname], tbl="A")
                                P.add("act", lambda e, chb=chb: e.activation(out=chb[:], in_=chb[:], func=AF.Exp, scale=-1.0),
                                      r=[kname], w=[kname])
                            P.next_c = 1.1
                            P.add("dve", lambda e: e.tensor_tensor(out=ch_r[:], in0=ch_r[:], in1=rbig[:], op=ALU.add),
                                  r=["ch_r", "rbig"], w=["ch_r"])
                            P.add("act", lambda e, cc=cc: e.activation(out=ch_a[:], in_=ch_r[:], func=AF.Exp,
                                                                       scale=cp[:, cc:cc + 1]),
                                  r=["ch_r", "cp"], w=["ch_a"])

                            P.add("act", lambda e, cc=cc: e.activation(out=ch_m[:], in_=ch_r[:], func=AF.Exp,
                                                                       scale=c2p[:, cc:cc + 1]), r=["ch_r", "c2p"], w=["ch_m"])
                            P.add("dve", lambda e: e.tensor_scalar(out=ch_m[:], in0=ch_m[:], scalar1=0.99999994, scalar2=None,
                                                                   op0=ALU.min), r=["ch_m"], w=["ch_m"])
                            P.add("act", lambda e: e.activation(out=ch_m[:], in_=ch_m[:], func=AF.Ln, scale=-1.0, bias=1.0),
                                  r=["ch_m"], w=["ch_m"], tbl="A")
                            P.add("act", lambda e: e.activation(out=ch_m[:], in_=ch_m[:], func=AF.Exp, scale=0.5, bias=LN_HALF),
                                  r=["ch_m"], w=["ch_m"])
                            P.next_c = 1.5
                            P.add("pool", lambda e, oc=oc: e.tensor_tensor(out=ch_b[:], in0=ch_i[:], in1=xc[:, rbb, oc, :],
                                                                           op=ALU.mult),
                                  r=["ch_i", "xc%d_%d" % (rbb, oc)], w=["ch_b"])
                            P.next_c = 1.1
                            P.add("dve", lambda e: e.tensor_tensor(out=ch_b[:], in0=ch_b[:], in1=ch_m[:], op=ALU.mult),
                                  r=["ch_b", "ch_m"], w=["ch_b"])

                            P.next_c = 1.1
                            P.add("dve", lambda e, cc=cc: e.tensor_tensor_scan(
                                out=ch_h[:], data0=ch_a[:], data1=ch_b[:], initial=hstate[:, cc:cc + 1], op0=ALU.mult, op1=ALU.add),
                                r=["ch_a", "ch_b", "hstate"], w=["ch_h"])
                            P.next_c = 0.1
                            P.add("dve", lambda e, cc=cc: e.tensor_copy(out=hstate[:, cc:cc + 1], in_=ch_h[:, TT - 1:TT]),
                                  r=["ch_h"], w=["hstate"])
                            P.next_c = 1.1
                            P.add("dve", lambda e, cc=cc: e.tensor_tensor(out=yrT[:, cc, :], in0=ch_h[:], in1=sgr[:, cc, :], op=ALU.mult),
                                  r=["ch_h", "sgr%d" % cc], w=["yrT"])
                feat_piece(pid, ev)

            if _STOP < 7:
                continue
            for hf in range(2):
                def ev_ma(j, b):
                    P.add("act", lambda e: e.activation(out=sgate[:, j, :], in_=ps[:, b, :], func=AF.Tanh, scale=0.5),
                          r=pkeys(b), w=["sgate%d" % j], tbl="B")
                feat_piece(PC_MA[hf], ev_ma)
                slot = load_piece(PC_WAP[hf])
                for j in range(4):
                    b = banks(1)

                    def mm(e, b=b, j=j, slot=slot):
                        ins = None
                        for kc in range(KC):
                            ins = e.matmul(ps[:, b, :], lhsT=wring[:, slot, kc, j * 128:(j + 1) * 128], rhs=yaT[:, kc, :],
                                           start=(kc == 0), stop=(kc == KC - 1))
                        return ins
                    P.add("pe", mm, r=["wr%d" % slot, "yaT"], w=pkeys(b))
                    P.add("dve", lambda e, b=b, j=j: e.scalar_tensor_tensor(out=t1m[:, j, :], in0=sgate[:, j, :], scalar=1.0,
                                                                            in1=ps[:, b, :], op0=ALU.add, op1=ALU.mult),
                          r=pkeys(b) + ["sgate%d" % j], w=["t1m%d" % j])
                feat_piece(PC_MR[hf], ev_ma)
                slot = load_piece(PC_WRP[hf])
                for j in range(4):
                    b = banks(1)
                    oc = hf * 4 + j

                    def mm(e, b=b, j=j, slot=slot):
                        ins = None
                        for kc in range(KC):
                            ins = e.matmul(ps[:, b, :], lhsT=wring[:, slot, kc, j * 128:(j + 1) * 128], rhs=yrT[:, kc, :],
                                           start=(kc == 0), stop=(kc == KC - 1))
                        return ins
                    P.add("pe", mm, r=["wr%d" % slot, "yrT"], w=pkeys(b))

                    P.add("dve", lambda e, b=b, j=j: e.scalar_tensor_tensor(out=sgate[:, j, :], in0=sgate[:, j, :], scalar=1.0,
                                                                            in1=ps[:, b, :], op0=ALU.add, op1=ALU.mult),
                          r=pkeys(b) + ["sgate%d" % j], w=["sgate%d" % j])
                    P.next_c = 1.1
                    P.add("dve", lambda e, j=j, oc=oc: e.tensor_tensor(out=mgT[:, oc, :], in0=sgate[:, j, :], in1=t1m[:, j, :],
                                                                       op=ALU.add),
                          r=["sgate%d" % j, "t1m%d" % j], w=["mgT", "qT0", "qT1", "qT2", "qT3"])

            if _STOP < 8:
                continue
            wo_slots = [load_piece(PC_WO[0]), load_piece(PC_WO[1])]
            for bl in range(4):
                gb = ti * 4 + bl
                buf = gb % 2
                P.add("sync", lambda e, gb=gb, buf=buf: e.dma_start(out=xnew[:, buf, :], in_=x_d[gb * 128:(gb + 1) * 128, :]),
                      w=["xnew%d" % buf], dma="xr%d" % buf)
                obk = []
                for hf in range(2):
                    b = banks(1)
                    obk.append(b)

                    def mm(e, b=b, bl=bl, slot=wo_slots[hf]):
                        ins = None
                        for kc in range(KC):
                            ins = e.matmul(ps[:, b, :], lhsT=mgT[:, kc, bl * 128:(bl + 1) * 128], rhs=wring[:, slot, kc, :],
                                           start=(kc == 0), stop=(kc == KC - 1))
                        return ins
                    P.add("pe", mm, r=["wr%d" % wo_slots[hf], "mgT"], w=pkeys(b))
                b0, b1 = obk

                def resid(e, b0=b0, b1=b1, buf=buf):
                    e.tensor_tensor(out=xnew[:, buf, 0:512], in0=ps[:, b0, :], in1=xnew[:, buf, 0:512], op=ALU.add)
                    return e.tensor_tensor(out=xnew[:, buf, 512:1024], in0=ps[:, b1, :], in1=xnew[:, buf, 512:1024], op=ALU.add)
                P.next_c = 1.2
                P.add("dve", resid, r=pkeys(b0) + pkeys(b1) + ["xnew%d" % buf], w=["xnew%d" % buf])
                sk = 4 + bl
                P.next_c = 1.0
                P.add("act", lambda e, buf=buf, sk=sk: e.activation(out=junk2[:], in_=xnew[:, buf, :], func=AF.Square,
                                                                    accum_out=ssq[:, sk:sk + 1]),
                      r=["xnew%d" % buf], w=["junk2", "ssq%d" % sk])
                P.next_c = 0.25
                P.add("pool", lambda e, sk=sk: e.tensor_scalar(out=ssq[:, sk:sk + 1], in0=ssq[:, sk:sk + 1], scalar1=1.0 / D,
                                                               scalar2=EPS, op0=ALU.mult, op1=ALU.add),
                      r=["ssq%d" % sk], w=["ssq%d" % sk])
                P.next_c = 1.7
                P.add("pool", lambda e, sk=sk: e.tensor_tensor(out=ssq[:, sk:sk + 1], in0=ssq[:, sk:sk + 1], in1=nhalf, op=ALU.pow),
                      r=["ssq%d" % sk, "nhalf"], w=["ssq%d" % sk])
                P.next_c = 2.2
                P.add("dve", lambda e, buf=buf, sk=sk: e.scalar_tensor_tensor(
                    out=xnew[:, buf, :], in0=xnew[:, buf, :], scalar=ssq[:, sk:sk + 1], in1=fgbc[:], op0=ALU.mult, op1=ALU.mult),
                    r=["xnew%d" % buf, "ssq%d" % sk, "fgbc"], w=["xnew%d" % buf])
                P.add("sync", lambda e, gb=gb, buf=buf: e.dma_start(out=y_d[gb * 128:(gb + 1) * 128, :], in_=xnew[:, buf, :]),
                      r=["xnew%d" % buf], w=["ydma%d" % buf, "yout%d" % gb], dma="yst%d" % buf)
        P.add("sync", None, r=["yout%d" % gb for gb in range(NB)] if _STOP >= 8 else [])

        with nc.Block() as block:
            P.emit(nc, block, st)
    return nc


_NC_CACHE = {}


def _host_layout(inputs, b):
    f32 = np.float32
    S = inputs["x"].shape[1]
    NB = S // 128
    pos = np.asarray(inputs["positions"][b]).astype(np.int32)
    pp = np.zeros((128, NPP), f32)
    pp[:, PP_CT:PP_CT + 8] = np.asarray(inputs["c"][b], f32).reshape(8, 128).T
    pp[:, PP_BADA:PP_BADA + 24] = np.asarray(inputs["b_ada"][0], f32).reshape(24, 128).T
    pp[:, PP_NG:PP_NG + 8] = np.asarray(inputs["norm_g"][0], f32).reshape(8, 128).T
    pp[:, PP_CW:PP_CW + 32] = np.asarray(inputs["conv_w"][0], f32).reshape(4, 8, 128).transpose(2, 1, 0).reshape(128, 32)
    pp[:, PP_CB:PP_CB + 8] = np.asarray(inputs["conv_b"][0], f32).reshape(8, 128).T
    pp[:, PP_BA:PP_BA + 8] = np.asarray(inputs["rg_ba"][0], f32).reshape(8, 128).T
    pp[:, PP_BX:PP_BX + 8] = np.asarray(inputs["rg_bx"][0], f32).reshape(8, 128).T
    pp[:, PP_LAM:PP_LAM + 8] = np.asarray(inputs["rg_lambda"][0], f32).reshape(8, 128).T
    sinks = np.asarray(inputs["attn_sinks"][0], f32)
    for c in range(8):
        pp[0:64, PP_SINK + c] = sinks[2 * c]
        pp[64:128, PP_SINK + c] = sinks[2 * c + 1]
    return dict(
        x=np.ascontiguousarray(np.asarray(inputs["x"][b], f32)),
        posT=np.ascontiguousarray(pos.reshape(NB, 128).T),
        posrow=np.ascontiguousarray(pos.reshape(1, S)),
        pp=pp,
    )


def kernel(**inputs):
    f32 = np.float32
    x = np.asarray(inputs["x"])
    B, S, _ = x.shape
    if S not in _NC_CACHE:
        _NC_CACHE[S] = build(S)
    nc = _NC_CACHE[S]
    invf = (np.float32(500000.0) ** (-np.arange(0, 16, 2, dtype=np.float32) / np.float32(16))).astype(f32)
    cbf = np.zeros((128, 128 + 1024 + 64), f32)
    cbf[:, 0:128] = np.eye(128, dtype=f32)
    s_idx = np.arange(128)[:, None]
    q_idx = np.arange(128)[None, :]
    off = (s_idx > q_idx).astype(f32)
    dia = (s_idx <= q_idx).astype(f32)
    half_mask = np.concatenate([off, off, dia, dia], axis=1)
    cbf[:, 128:128 + 512] = half_mask
    cbf[:, 128 + 512:128 + 1024] = half_mask
    cbf[:, 1152:1216] = 1.0
    cbf = cbf.astype(ml_dtypes.bfloat16)
    rows = np.stack([np.asarray(inputs["b_ada"][0], f32)[2048:3072], np.asarray(inputs["final_g"], f32)])
    rgw = np.stack([np.asarray(inputs["rg_wa"][0], f32), np.asarray(inputs["rg_wx"][0], f32)])
    rgw = rgw.reshape(2, 4, 2, 128, 256).transpose(3, 0, 1, 2, 4).reshape(128, 8, 512)
    shared = dict(
        rows=np.ascontiguousarray(rows), cbf=cbf, rgw=np.ascontiguousarray(rgw),
        w_ada=np.ascontiguousarray(np.asarray(inputs["w_ada"][0], f32)),
        w_in=np.ascontiguousarray(np.asarray(inputs["w_in"][0], f32)),
        w_ap=np.ascontiguousarray(np.asarray(inputs["w_attn_proj"][0], f32)),
        w_rp=np.ascontiguousarray(np.asarray(inputs["w_rnn_proj"][0], f32)),
        w_o=np.ascontiguousarray(np.asarray(inputs["w_out"][0], f32)),
    )
    in_maps = []
    for b in range(B):
        m = _host_layout(inputs, b)
        m["pp"][:, PP_INVF:PP_INVF + 8] = invf[None, :]
        m.update(shared)
        in_maps.append(m)
    res = run_bass_kernel_spmd(nc, in_maps, core_ids=list(range(B)))
    out = np.stack([np.asarray(res.results[b]["y"], f32) for b in range(B)], axis=0)
    return out
```
